# Optimizing a Trainium2 kernel written in Bass

```python
import jax, jax.numpy as jnp
from jax import lax
import numpy as np

D_MODEL = 2048
BATCH = 1
SEQ = 16384
DEPTH = 4
DEC_BATCH = 16
DEC_SEQ = 2048
PAST_LEN = 128

N_MIXERS = 2
N_A_LAYERS = (DEPTH + 1) // 2
N_B_LAYERS = DEPTH // 2
CONV_WIDTH = 3
CONV_DIM = D_MODEL
N_FGROUPS = 8
FGROUP_DIM = D_MODEL // N_FGROUPS
D_FF = 4 * D_MODEL
RMS_EPS = 1e-6

kernel_name = "hybrid_conv_fourier_encoder"


def _rmsnorm(x, g):
    xf = x.astype(jnp.float32)
    y = xf * lax.rsqrt(jnp.mean(xf * xf, axis=-1, keepdims=True) + RMS_EPS)
    return (y * g.astype(jnp.float32)).astype(x.dtype)


def _short_conv_mixer(h, w_in, conv_w, w_out):
    s = h.shape[1]
    b_gate, c_gate, v = jnp.split(h @ w_in, 3, axis=-1)
    u = c_gate * v
    u_pad = jnp.pad(u, ((0, 0), (1, 1), (0, 0)))
    conv = (conv_w[0] * u_pad[:, 0:s]
            + conv_w[1] * u_pad[:, 1:s + 1]
            + conv_w[2] * u_pad[:, 2:s + 2])
    return (b_gate * conv) @ w_out


def _fourier_mixer(h, w_out):
    b, s, d = h.shape
    hg = h.astype(jnp.float32).reshape(b, s, N_FGROUPS, FGROUP_DIM)
    mixed = jnp.fft.fftn(hg, axes=(1, 3), norm="ortho").real
    return mixed.reshape(b, s, d).astype(h.dtype) @ w_out


def _mlp(h, w_up, w_down):
    return jnp.square(jax.nn.relu(h @ w_up)) @ w_down


def _trunk(x, norm_mix, a_w_in, a_conv_w, a_w_out, f_w_out, norm_ffn, w_up, w_down, final_norm):
    for i in range(DEPTH):
        h = _rmsnorm(x, norm_mix[i])
        j = i // N_MIXERS
        if i % N_MIXERS == 0:
            x = x + _short_conv_mixer(h, a_w_in[j], a_conv_w[j], a_w_out[j])
        else:
            x = x + _fourier_mixer(h, f_w_out[j])
        x = x + _mlp(_rmsnorm(x, norm_ffn[i]), w_up[i], w_down[i])
    return _rmsnorm(x, final_norm)


def setup_inputs(seed: int = 0) -> dict:
    key = jax.random.key(seed)
    ks = jax.random.split(key, 12)
    f32 = jnp.float32
    d = D_MODEL
    x_prompt = jax.random.normal(ks[0], (BATCH, SEQ, d), f32)
    x_sample = jax.random.normal(ks[1], (DEC_BATCH, DEC_SEQ, d), f32)
    norm_mix = 1.0 + 0.02 * jax.random.normal(ks[2], (DEPTH, d), f32)
    a_w_in = jax.random.normal(ks[3], (N_A_LAYERS, d, 3 * CONV_DIM), f32) * d ** -0.5
    a_conv_w = jax.random.normal(ks[4], (N_A_LAYERS, CONV_WIDTH, CONV_DIM), f32) * CONV_WIDTH ** -0.5
    a_w_out = jax.random.normal(ks[5], (N_A_LAYERS, CONV_DIM, d), f32) * CONV_DIM ** -0.5
    f_w_out = jax.random.normal(ks[6], (N_B_LAYERS, d, d), f32) * d ** -0.5
    norm_ffn = 1.0 + 0.02 * jax.random.normal(ks[7], (DEPTH, d), f32)
    w_up = jax.random.normal(ks[8], (DEPTH, d, D_FF), f32) * d ** -0.5
    w_down = jax.random.normal(ks[9], (DEPTH, D_FF, d), f32) * D_FF ** -0.5
    final_norm = 1.0 + 0.02 * jax.random.normal(ks[10], (d,), f32)
    return {"x_prompt": x_prompt, "x_sample": x_sample, "norm_mix": norm_mix,
            "a_w_in": a_w_in, "a_conv_w": a_conv_w, "a_w_out": a_w_out,
            "f_w_out": f_w_out, "norm_ffn": norm_ffn, "w_up": w_up,
            "w_down": w_down, "final_norm": final_norm}


def reference(x_prompt, x_sample, norm_mix, a_w_in, a_conv_w, a_w_out, f_w_out,
              norm_ffn, w_up, w_down, final_norm):
    y_prompt = _trunk(x_prompt, norm_mix, a_w_in, a_conv_w, a_w_out, f_w_out,
                      norm_ffn, w_up, w_down, final_norm)
    y_sample = _trunk(x_sample, norm_mix, a_w_in, a_conv_w, a_w_out, f_w_out,
                      norm_ffn, w_up, w_down, final_norm)
    return (y_prompt, y_sample)
```

```python
import numpy as np
from contextlib import ExitStack

import concourse.bass as bass
import concourse.mybir as mybir
from concourse.bass_utils import run_bass_kernel_spmd

F32 = mybir.dt.float32
BF16 = mybir.dt.bfloat16
ALU = mybir.AluOpType
AF = mybir.ActivationFunctionType

D = 2048
KC = 16
T = 512
NSLOT = 4
SLOT = 8192
EPS = 1e-6


class Sem:
    def __init__(self, nc, es, name):
        self.h = es.enter_context(nc.semaphore(name))
        self.n = 0

    def inc(self, ins, k=1):
        ins.then_inc(self.h, k)
        self.n += k
        return (self, self.n)

    def dma(self, ins):
        return self.inc(ins, 16)


def W(eng, tok):
    if tok is not None:
        eng.wait_ge(tok[0].h, tok[1])


def build(SP, SS, NS, DFF, DEPTH, NSPLIT, TAIL_SPLIT=True):
    if not TAIL_SPLIT:
        NSPLIT = 1
    nc = bass.Bass("TRN2", target_bir_lowering=False)
    NA = (DEPTH + 1) // 2
    NB = max(DEPTH // 2, 1)
    NT = SP + NS * SS
    NFB = DFF // 512
    N1P, N1S = SP // 128, SS // 128
    BLK = SP // NSPLIT

    def din(name, shape, dt=F32):
        return nc.dram_tensor(name, list(shape), dt, kind="ExternalInput").ap()

    x_prompt = din("x_prompt", [SP, D])
    x_sample = din("x_sample", [NS * SS, D])
    norm_mix = din("norm_mix", [DEPTH, D])
    a_w_in = din("a_w_in", [NA, D, 3 * D])
    a_conv_w = din("a_conv_w", [NA, 3, D])
    a_w_out = din("a_w_out", [NA, D, D])
    f_w_out = din("f_w_out", [NB, D, D])
    norm_ffn = din("norm_ffn", [DEPTH, D])
    w_up = din("w_up", [DEPTH, D, DFF])
    w_down = din("w_down", [DEPTH, DFF, D])
    final_norm = din("final_norm", [1, D])
    c_ident = din("c_ident", [128, 128])
    c_ccsc = din("c_ccsc", [256, 512])
    c_ra_p = din("c_ra_p", [N1P, 4 * N1P])
    c_ra_s = din("c_ra_s", [N1S, 4 * N1S])
    c_tw_p = din("c_tw_p", [128, 2 * N1P])
    c_tw_s = din("c_tw_s", [128, 2 * N1S])
    c_c2s2 = din("c_c2s2", [128, 256])
    c_sel = din("c_sel", [128, max(NSPLIT, 1)])
    y_prompt = nc.dram_tensor("y_prompt", [SP // NSPLIT, D], F32, kind="ExternalOutput").ap()
    y_sample = nc.dram_tensor("y_sample", [NS * SS, D], F32, kind="ExternalOutput").ap()

    R = [nc.dram_tensor("res%d" % i, [NT, D], F32).ap() for i in range(2)]
    PQ = nc.dram_tensor("pq", [32, NT, 128], BF16).ap()
    MX = nc.dram_tensor("mx", [32, NT, 64], BF16).ap()
    MXT = nc.dram_tensor("mxt", [32, SP // NSPLIT, 64], BF16).ap()
    RT = nc.dram_tensor("rtail", [SP // NSPLIT, D], F32).ap()
    wb_in = nc.dram_tensor("wb_in", [NA, 16, D, 384], BF16).ap()
    wb_aout = nc.dram_tensor("wb_aout", [NA, D, D], BF16).ap()
    wb_fout = nc.dram_tensor("wb_fout", [NB, D, D], BF16).ap()
    wb_up = nc.dram_tensor("wb_up", [DEPTH, D, DFF], BF16).ap()
    wb_down = nc.dram_tensor("wb_down", [DEPTH, DFF, D], BF16).ap()

    seqs = [(x_prompt, 0, SP, N1P, y_prompt)]
    for i in range(NS):
        seqs.append((x_sample[i * SS:(i + 1) * SS, :], SP + i * SS, SS, N1S, y_sample[i * SS:(i + 1) * SS, :]))

    es = ExitStack()
    with es:
        def sb(name, shape, dt):
            return es.enter_context(nc.sbuf_tensor(name, list(shape), dt))

        def ps(name, shape, dt):
            return es.enter_context(nc.psum_tensor(name, list(shape), dt))

        x = sb("x", [128, 4, D], F32)
        gb = sb("gb", [128, D], F32)
        junk = sb("junk", [128, D], BF16)
        hn = sb("hn", [128, D], BF16)
        hT = sb("hT", [128, KC, T + 2], BF16)
        act2 = sb("act2", [128, KC, T], BF16)
        hid = sb("hid", [128, 2, 4, T], BF16)
        tmpr = sb("tmpr", [128, 2, T], F32)
        ftmp = sb("ftmp", [128, 4, T + 2], F32)
        ring = sb("ring", [128, NSLOT * SLOT], BF16)
        io = sb("io", [128, 2 * D], BF16)
        bri = sb("bri", [128, 2, 2, 512], BF16)
        ss = sb("ss", [128, 32], F32)
        idb = sb("idb", [128, 128], BF16)
        ccsc = sb("ccsc", [128, 2, 512], BF16)
        ra_p = sb("ra_p", [128, 4 * N1P], BF16)
        ra_s = sb("ra_s", [128, 4 * N1S], BF16)
        tw_p = sb("tw_p", [128, 2 * N1P], F32)
        tw_s = sb("tw_s", [128, 2 * N1S], F32)
        c2s2 = sb("c2s2", [128, 256], BF16)
        cwt = sb("cwt", [128, NA * 3 * KC], F32)
        selt = sb("selt", [128, max(NSPLIT, 1)], F32)

        psU = [ps("psU%d" % i, [128, 512], F32) for i in range(2)]
        psD = [ps("psD%d" % i, [128, 512], F32) for i in range(3)]
        psX = ps("psX", [128, 512], F32)
        psT = [ps("psT%d" % i, [128, 8, 128], BF16) for i in range(2)]

        io_f32 = io[:, :].bitcast(F32)
        xh = io_f32

        S = {n: Sem(nc, es, n) for n in
             ["cast", "cst", "xld", "xst", "gbl", "pe", "act", "dve", "pool", "hld", "iost", "fld", "fst"]}
        ring_ld = [Sem(nc, es, "rl%d" % i) for i in range(NSLOT)]
        PE, ACT, DVE, POOL, SY = nc.tensor, nc.scalar, nc.vector, nc.gpsimd, nc.sync
        dynv = {}

        def prow(ap, r0, n):
            return ap.rearrange("(b t) d -> b t d", t=BLK)[bass.ds(dynv["pid"], 1), r0:r0 + n, :]

        POOL.dma_start(out=idb[:], in_=c_ident[:, :]).then_inc(S["cst"].h, 16)
        POOL.dma_start(out=ccsc[:], in_=c_ccsc.rearrange("(k p) c -> p k c", p=128)).then_inc(S["cst"].h, 16)
        POOL.dma_start(out=ra_p[0:N1P, :], in_=c_ra_p[:, :]).then_inc(S["cst"].h, 16)
        POOL.dma_start(out=ra_s[0:N1S, :], in_=c_ra_s[:, :]).then_inc(S["cst"].h, 16)
        POOL.dma_start(out=c2s2[:], in_=c_c2s2[:, :]).then_inc(S["cst"].h, 16)
        SY.dma_start(out=tw_p[:], in_=c_tw_p[:, :]).then_inc(S["cst"].h, 16)
        SY.dma_start(out=tw_s[:], in_=c_tw_s[:, :]).then_inc(S["cst"].h, 16)
        SY.dma_start(out=selt[:], in_=c_sel[:, :]).then_inc(S["cst"].h, 16)
        for j in range(NA):
            for t in range(3):
                o = (j * 3 + t) * KC
                SY.dma_start(out=cwt[:, o:o + KC], in_=a_conv_w[j, t:t + 1, :].rearrange("o (c p) -> p (o c)", p=128),
                             allow_slow_non_contiguous=True).then_inc(S["cst"].h, 16)
        S["cst"].n = 16 * (8 + 3 * NA)
        cst_tok = (S["cst"], S["cst"].n)
        for e in (PE, ACT, DVE, POOL):
            W(e, cst_tok)

        cast_tok = {}

        def cast2d(key, dst, src, rows, cols):
            for r0 in range(0, rows, 128):
                for c0 in range(0, cols, 2048):
                    c1 = min(cols, c0 + 2048)
                    t = S["cast"].dma(POOL.dma_start(out=dst[r0:r0 + 128, c0:c1], in_=src[r0:r0 + 128, c0:c1]))
            cast_tok[key] = t

        for L in range(DEPTH):
            j = L // 2
            if L % 2 == 0:
                for cc in range(KC):
                    for r0 in range(0, D, 128):
                        t = S["cast"].dma(POOL.dma_start(
                            out=wb_in[j, cc, r0:r0 + 128, :].rearrange("k (g c) -> k g c", g=3),
                            in_=a_w_in[j, r0:r0 + 128, :].rearrange("k (g c) -> k g c", g=3)[:, :, cc * 128:(cc + 1) * 128]))
                cast_tok[("in", L)] = t
                cast2d(("mo", L), wb_aout[j], a_w_out[j], D, D)
            else:
                cast2d(("mo", L), wb_fout[j], f_w_out[j], D, D)
            cast2d(("up", L), wb_up[L], w_up[L], D, DFF)
            cast2d(("dn", L), wb_down[L], w_down[L], DFF, D)

        plan = []
        state = {"issued": 0, "used": 0, "free_tok": {}, "cast_waited": set()}

        def ring_plan(ap, cast_key):
            plan.append((ap, cast_key))

        def ring_issue_upto(b):
            while state["issued"] <= min(b, len(plan) - 1):
                i = state["issued"]
                ap, ck = plan[i]
                slot = i % NSLOT
                if ck not in state["cast_waited"]:
                    W(SY, cast_tok[ck])
                    state["cast_waited"].add(ck)
                if i >= NSLOT:
                    W(SY, state["free_tok"][i - NSLOT])
                shp = ap.shape
                n = shp[1] * shp[2]
                dst = ring[:, slot * SLOT: slot * SLOT + n].rearrange("p (a b) -> p a b", a=shp[1])
                ring_ld[slot].dma(SY.dma_start(out=dst, in_=ap))
                state["issued"] += 1

        def ring_next():
            b = state["used"]
            ring_issue_upto(b + NSLOT - 1)
            slot = b % NSLOT
            ap, _ = plan[b]
            shp = ap.shape
            n = shp[1] * shp[2]
            view = ring[:, slot * SLOT: slot * SLOT + n].rearrange("p (a b) -> p a b", a=shp[1])
            PE.wait_ge(ring_ld[slot].h, 16 * (b // NSLOT + 1))
            state["used"] += 1
            return b, view

        def ring_free(b, tok):
            state["free_tok"][b] = tok

        tk = {k: None for k in ["x_free", "x_ld", "gb_ld", "gb_free", "hn_free", "psT0", "psT1", "hT_free", "act2_free",
                                 "io_free", "psU0", "psU1", "psD0", "psD1", "psD2", "psX", "tmpr0", "tmpr1",
                                 "hid0", "hid1", "ftmp_free", "bg_free", "u_free", "hnld_free", "bri0", "bri1", "xh_ld", "junk_free"]}
        cnt = {"psD": 0, "psU": 0}

        def load_gb(vec_ap):
            W(SY, tk["gb_free"])
            tk["gb_ld"] = S["gbl"].dma(SY.dma_start(out=gb[:], in_=vec_ap.partition_broadcast(128)))

        def transposes(src_tile, s, dst, ncols, width=128):
            last = None
            for half in range(2):
                W(PE, tk["psT%d" % half])
                for c in range(8):
                    ins = PE.transpose(psT[half][:, c, :], src_tile[:, (half * 8 + c) * 128:(half * 8 + c + 1) * 128], idb[:])
                tp = S["pe"].inc(ins)
                W(ACT, tp)
                ins = ACT.copy(dst[:, half * 8:half * 8 + 8, ncols:ncols + width], psT[half][:, :, 0:width])
                tk["psT%d" % half] = S["act"].inc(ins)
                last = tp
            return last, tk["psT1"]

        def norm_stats(nsub, srcs):
            t0 = S["dve"].inc(DVE.memset(ss[:, 8:8 + nsub], 0.0))
            W(ACT, t0)
            W(ACT, tk["x_ld"])
            W(ACT, tk["junk_free"])
            for s in range(nsub):
                ins = ACT.activation(junk[:], srcs[s], AF.Square, accum_out=ss[:, 8 + s:9 + s])
            ta = S["act"].inc(ins)
            W(DVE, ta)
            tv = S["dve"].inc(DVE.tensor_scalar(ss[:, 16:16 + nsub], ss[:, 8:8 + nsub], 1.0 / D, EPS, op0=ALU.mult, op1=ALU.add))
            W(ACT, tv)
            ta = S["act"].inc(ACT.sqrt(ss[:, 24:24 + nsub], ss[:, 16:16 + nsub]))
            W(DVE, ta)
            tv = S["dve"].inc(DVE.reciprocal(ss[:, 0:nsub], ss[:, 24:24 + nsub]))
            W(DVE, tv)
            return tv

        def norm_to_hT(halo):
            nsub = 5 if halo else 4
            srcs = [x[:, s, :] for s in range(4)] + ([xh] if halo else [])
            if halo:
                W(ACT, tk["xh_ld"])
            norm_stats(nsub, srcs)
            W(DVE, tk["gb_ld"])
            W(PE, tk["hT_free"])
            W(ACT, tk["hT_free"])
            lastE = None
            for s in range(nsub):
                W(DVE, tk["hn_free"])
                th = S["dve"].inc(DVE.scalar_tensor_tensor(hn[:], srcs[s], ss[:, s:s + 1], gb[:], op0=ALU.mult, op1=ALU.mult))
                W(PE, th)
                if s < 4:
                    tp, lastE = transposes(hn, s, hT, s * 128, 128)
                else:
                    tp, lastE = transposes(hn, s, hT, T, 2)
                tk["hn_free"] = tp
            tk["gb_free"] = th
            return lastE

        def psD_next():
            i = cnt["psD"] % 3
            cnt["psD"] += 1
            return i

        def out_proj(src, wkey, wdram):
            last = None
            for db in range(4):
                b, wv = ring_next()
                for s in range(4):
                    i = psD_next()
                    W(PE, tk["psD%d" % i])
                    for cc in range(KC):
                        ins = PE.matmul(psD[i][:], src[:, cc, s * 128:(s + 1) * 128], wv[:, cc, :], start=(cc == 0), stop=(cc == KC - 1))
                    tp = S["pe"].inc(ins)
                    W(DVE, tp)
                    ins = DVE.tensor_tensor(x[:, s, db * 512:(db + 1) * 512], psD[i][:], x[:, s, db * 512:(db + 1) * 512], ALU.add)
                    tk["psD%d" % i] = S["dve"].inc(ins)
                    last = tk["psD%d" % i]
                ring_free(b, tp)
            return tp, last

        def plan_out_proj(wdram, key):
            for db in range(4):
                ring_plan(wdram.rearrange("(cc p) d -> p cc d", p=128)[:, :, db * 512:(db + 1) * 512], key)

        def plan_mlp(L):
            up = lambda fb: ring_plan(wb_up[L].rearrange("(kc p) f -> p kc f", p=128)[:, :, fb * 512:(fb + 1) * 512], ("up", L))
            dn = lambda fb: ring_plan(wb_down[L][fb * 512:(fb + 1) * 512, :].rearrange("(fc p) d -> p fc d", p=128), ("dn", L))
            up(0)
            for fb in range(NFB):
                if fb + 1 < NFB:
                    up(fb + 1)
                dn(fb)

        def mlp(L, hT_ready):
            W(PE, hT_ready)
            res = {"last_add": None, "tp": None}

            def up_block(fb):
                hb = fb % 2
                bu, wu = ring_next()
                tsq = None
                for fc in range(4):
                    ui = cnt["psU"] % 2
                    cnt["psU"] += 1
                    W(PE, tk["psU%d" % ui])
                    for kc in range(KC):
                        ins = PE.matmul(psU[ui][:], wu[:, kc, fc * 128:(fc + 1) * 128], hT[:, kc, 0:T], start=(kc == 0), stop=(kc == KC - 1))
                    tp = S["pe"].inc(ins)
                    W(ACT, tp)
                    W(ACT, tk["tmpr%d" % ui])
                    ta = S["act"].inc(ACT.activation(tmpr[:, ui, :], psU[ui][:], AF.Relu))
                    tk["psU%d" % ui] = ta
                    W(POOL, ta)
                    if fc == 0:
                        W(POOL, tk["hid%d" % hb])
                    tsq = S["pool"].inc(POOL.tensor_tensor(hid[:, hb, fc, :], tmpr[:, ui, :], tmpr[:, ui, :], ALU.mult))
                    tk["tmpr%d" % ui] = tsq
                ring_free(bu, tp)
                res["tp_up"] = tp
                return tsq

            def down_block(fb, tsq):
                hb = fb % 2
                bd, wd = ring_next()
                W(PE, tsq)
                tp = None
                for s in range(4):
                    for db in range(4):
                        i = psD_next()
                        W(PE, tk["psD%d" % i])
                        for fc in range(4):
                            ins = PE.matmul(psD[i][:], hid[:, hb, fc, s * 128:(s + 1) * 128], wd[:, fc, db * 512:(db + 1) * 512],
                                            start=(fc == 0), stop=(fc == 3))
                        tp = S["pe"].inc(ins)
                        W(DVE, tp)
                        ins = DVE.tensor_tensor(x[:, s, db * 512:(db + 1) * 512], psD[i][:], x[:, s, db * 512:(db + 1) * 512], ALU.add)
                        tk["psD%d" % i] = S["dve"].inc(ins)
                        res["last_add"] = tk["psD%d" % i]
                ring_free(bd, tp)
                tk["hid%d" % hb] = tp
                res["tp"] = tp

            tsqs = {0: up_block(0)}
            for fb in range(NFB):
                if fb + 1 < NFB:
                    tsqs[fb + 1] = up_block(fb + 1)
                down_block(fb, tsqs[fb])
            tk["hT_free"] = res["tp"]
            return res["last_add"]

        def src_rows(L, q, r0, n):
            xin, off, Sq, N1, yout = seqs[q]
            if L == 0:
                return xin[r0:r0 + n, :]
            return R[L % 2][off + r0:off + r0 + n, :]

        def dst_rows(L, q, r0, n):
            xin, off, Sq, N1, yout = seqs[q]
            return R[(L + 1) % 2][off + r0:off + r0 + n, :]

        def load_x(L, q, g0, dyn=False):
            W(SY, tk["x_free"])
            for s in range(4):
                src = RT[g0 + s * 128:g0 + (s + 1) * 128, :] if dyn else src_rows(L, q, g0 + s * 128, 128)
                t = S["xld"].dma(SY.dma_start(out=x[:, s, :], in_=src))
            tk["x_ld"] = t

        def store_x(L, q, g0, done_tok):
            W(SY, done_tok)
            for s in range(4):
                t = S["xst"].dma(SY.dma_start(out=dst_rows(L, q, g0 + s * 128, 128), in_=x[:, s, :]))
            W(SY, t)
            tk["x_free"] = t

        def final_out(q, g0, done_tok, dyn=False):
            xin, off, Sq, N1, yout = seqs[q]
            load_gb(final_norm[0:1, :])
            W(ACT, done_tok)
            W(DVE, done_tok)
            norm_stats(4, [x[:, s, :] for s in range(4)])
            W(DVE, tk["gb_ld"])
            t = None
            for s in range(4):
                W(DVE, tk["io_free"])
                tv = S["dve"].inc(DVE.scalar_tensor_tensor(io_f32, x[:, s, :], ss[:, s:s + 1], gb[:], op0=ALU.mult, op1=ALU.mult))
                W(SY, tv)
                dsta = yout[g0 + s * 128:g0 + (s + 1) * 128, :]
                t = S["iost"].dma(SY.dma_start(out=dsta, in_=io_f32))
                tk["io_free"] = t
            tk["gb_free"] = tv
            W(SY, t)
            tk["x_free"] = t

        def conv_group(L, q, g0):
            j = L // 2
            xin, off, Sq, N1, yout = seqs[q]
            load_x(L, q, g0)
            W(POOL, tk["io_free"])
            tz = S["pool"].inc(POOL.memset(xh[:, :], 0.0))
            W(SY, tz)
            t = tz
            if g0 > 0:
                t = S["hld"].dma(SY.dma_start(out=xh[0:1, :], in_=src_rows(L, q, g0 - 1, 1)))
            if g0 + T < Sq:
                t = S["hld"].dma(SY.dma_start(out=xh[1:2, :], in_=src_rows(L, q, g0 + T, 1)))
            tk["xh_ld"] = t
            load_gb(norm_mix[L:L + 1, :])
            for cc in range(KC):
                ring_plan(wb_in[j, cc].rearrange("(kc p) c -> p kc c", p=128), ("in", L))
            plan_out_proj(wb_aout[j], ("mo", L))
            plan_mlp(L)
            hT_ready = norm_to_hT(True)
            tk["io_free"] = hT_ready
            W(PE, hT_ready)
            Bg, Cg, u, acc = ftmp[:, 0, 0:T], ftmp[:, 1, :], ftmp[:, 2, :], ftmp[:, 3, 0:T]
            psB, psC, psV, psH = psU[0], psU[1], psD[0], psX
            W(PE, tk["psD0"]); W(PE, tk["psU0"]); W(PE, tk["psU1"])
            W(POOL, tk["act2_free"])
            tpool = None
            tAB = tACm = tu = t1 = None
            hTm, hTh = hT[:, :, 0:T], hT[:, :, T:T + 2]

            def mm16(pt, wv, c0, rhs):
                for kc in range(KC):
                    ins = PE.matmul(pt, wv[:, kc, c0:c0 + 128], rhs[:, kc, :], start=(kc == 0), stop=(kc == KC - 1))
                return ins

            for cc in range(KC):
                b, wv = ring_next()
                W(PE, tAB)
                tB = S["pe"].inc(mm16(psB[:], wv, 0, hTm))
                W(PE, tACm)
                tC = S["pe"].inc(mm16(psC[:], wv, 128, hTm))
                W(PE, tu)
                mm16(psH[:, 0:2], wv, 128, hTh)
                mm16(psV[:], wv, 256, hTm)
                tp = S["pe"].inc(mm16(psH[:, 2:4], wv, 256, hTh))
                ring_free(b, tp)
                W(ACT, tB)
                W(ACT, tpool)
                tAB = S["act"].inc(ACT.copy(Bg, psB[:]))
                W(ACT, tC)
                W(ACT, tu)
                tACm = S["act"].inc(ACT.copy(Cg[:, 1:T + 1], psC[:]))
                W(ACT, tp)
                ACT.copy(Cg[:, 0:1], psH[:, 0:1])
                tACh = S["act"].inc(ACT.copy(Cg[:, T + 1:T + 2], psH[:, 1:2]))
                W(DVE, tACh)
                W(DVE, tpool)
                DVE.tensor_tensor(u[:, 1:T + 1], psV[:], Cg[:, 1:T + 1], ALU.mult)
                DVE.tensor_tensor(u[:, 0:1], psH[:, 2:3], Cg[:, 0:1], ALU.mult)
                tu = S["dve"].inc(DVE.tensor_tensor(u[:, T + 1:T + 2], psH[:, 3:4], Cg[:, T + 1:T + 2], ALU.mult))
                o = (j * 3) * KC + cc
                W(DVE, tu)
                t1 = S["dve"].inc(DVE.tensor_scalar(acc, u[:, 0:T], cwt[:, o:o + 1], None, op0=ALU.mult))
                W(DVE, t1)
                t1 = S["dve"].inc(DVE.scalar_tensor_tensor(acc, u[:, 1:T + 1], cwt[:, o + KC:o + KC + 1], acc, op0=ALU.mult, op1=ALU.add))
                W(DVE, t1)
                t1 = S["dve"].inc(DVE.scalar_tensor_tensor(acc, u[:, 2:T + 2], cwt[:, o + 2 * KC:o + 2 * KC + 1], acc, op0=ALU.mult, op1=ALU.add))
                W(POOL, t1)
                W(POOL, tAB)
                tpool = S["pool"].inc(POOL.tensor_tensor(act2[:, cc, :], acc, Bg, ALU.mult))
            tk["psU0"] = tAB; tk["psU1"] = tACm; tk["psD0"] = tu; tk["psX"] = tu
            tk["hT_free"] = tp
            W(PE, tpool)
            W(DVE, tk["x_ld"])
            tp, last = out_proj(act2, ("mo", L), wb_aout[j])
            tk["act2_free"] = tp
            return last

        def ffn_part(L, q, g0, mix_done, dyn=False):
            load_gb(norm_ffn[L:L + 1, :])
            W(ACT, mix_done)
            W(DVE, mix_done)
            hT_ready = norm_to_hT(False)
            last_add = mlp(L, hT_ready)
            if L == DEPTH - 1:
                final_out(q, g0, last_add, dyn)
            else:
                done = last_add
                if (L + 1) % 2 == 1:
                    done = fourier_a_body(L + 1, q, g0, last_add)
                store_x(L, q, g0, done)

        def fourier_a_body(L, q, g0, x_done):
            xin, off, Sq, N1, yout = seqs[q]
            load_gb(norm_mix[L:L + 1, :])
            W(ACT, x_done)
            W(DVE, x_done)
            hT_ready = norm_to_hT(False)
            x_read_done = tk["gb_free"]
            W(PE, hT_ready)
            iov = io[:, :].rearrange("p (cb h c) -> p cb h c", cb=32, h=2)
            tp = None
            for s in range(4):
                W(ACT, tk["io_free"])
                for g8 in range(8):
                    ui = cnt["psU"] % 2
                    cnt["psU"] += 1
                    W(PE, tk["psU%d" % ui])
                    for h in range(2):
                        for kk in range(2):
                            ins = PE.matmul(psU[ui][:, h * 256:(h + 1) * 256], hT[:, 2 * g8 + kk, s * 128:(s + 1) * 128],
                                            ccsc[:, kk, h * 256:(h + 1) * 256], start=(kk == 0), stop=(kk == 1))
                    tp = S["pe"].inc(ins)
                    W(ACT, tp)
                    ins = ACT.copy(iov[:, 4 * g8:4 * g8 + 4, :, :].rearrange("p cb h c -> p h cb c"),
                                   psU[ui][:].rearrange("p (h cb c) -> p h cb c", h=2, cb=4))
                    tk["psU%d" % ui] = S["act"].inc(ins)
                W(SY, tk["psU%d" % ui])
                r0 = off + g0 + s * 128
                t = S["iost"].dma(SY.dma_start(out=PQ[:, r0:r0 + 128, :].rearrange("cb t c -> t cb c"),
                                               in_=io[:, :].rearrange("p (cb c) -> p cb c", cb=32)))
                tk["io_free"] = t
            tk["hT_free"] = tp
            return x_read_done

        def fft_unit(q, cb):
            xin, off, Sq, N1, yout = seqs[q]
            ra = ra_p if N1 == N1P else ra_s
            tw = tw_p if N1 == N1P else tw_s
            if N1P == N1S:
                ra, tw = ra_p, tw_p
            nchA = 256 // N1
            nchC = 2 * nchA
            pz = ring[:, 0:2 * SLOT].rearrange("p (n h c) -> p n h c", n=128, h=2)
            mxv = ring[:, 2 * SLOT:2 * SLOT + N1 * 64].rearrange("p (k c) -> p k c", c=64)
            W(SY, tk["fft_pz_free"])
            t = S["fld"].dma(SY.dma_start(out=pz[0:N1], in_=PQ[cb, off:off + Sq, :].rearrange("(n1 n2) (h c) -> n1 n2 h c", n2=128, h=2)))
            W(PE, t)
            trb = tw[:, 0:N1].unsqueeze(1).unsqueeze(1).broadcast_to([128, nchA, 2, N1])
            tib = tw[:, N1:2 * N1].unsqueeze(1).unsqueeze(1).broadcast_to([128, nchA, 2, N1])
            t1v = ftmp[:, 0, 0:512].rearrange("p (c r k) -> p c r k", c=nchA, r=2)
            t2v = ftmp[:, 1, 0:512].rearrange("p (c r k) -> p c r k", c=nchA, r=2)
            W(ACT, tk["fft_mx_free"])
            tp = None
            for ci, c0 in enumerate(range(0, 64, nchC)):
                bset = ci % 2
                tpool = None
                for hh in range(2):
                    ui = cnt["psU"] % 2
                    cnt["psU"] += 1
                    W(PE, tk["psU%d" % ui])
                    for jc in range(nchA):
                        ch = c0 + hh * nchA + jc
                        PE.matmul(psU[ui][:, jc * 2 * N1:(jc + 1) * 2 * N1], pz[0:N1, :, 0, ch], ra[0:N1, 0:2 * N1], start=True, stop=False)
                        ins = PE.matmul(psU[ui][:, jc * 2 * N1:(jc + 1) * 2 * N1], pz[0:N1, :, 1, ch], ra[0:N1, 2 * N1:4 * N1], start=False, stop=True)
                    tp = S["pe"].inc(ins)
                    W(DVE, tp)
                    W(DVE, tpool if tpool is not None else tk["ftmp_free"])
                    av = psU[ui][:].rearrange("p (c r k) -> p c r k", c=nchA, r=2)
                    DVE.tensor_tensor(t1v, av, trb, ALU.mult)
                    tv = S["dve"].inc(DVE.tensor_tensor(t2v, av, tib, ALU.mult))
                    tk["psU%d" % ui] = tv
                    W(DVE, tv)
                    if hh == 0:
                        W(DVE, tk["bri%d" % bset])
                    brv = bri[:, bset, 0, hh * 256:(hh + 1) * 256].rearrange("p (c k) -> p c k", c=nchA)
                    biv = bri[:, bset, 1, hh * 256:(hh + 1) * 256].rearrange("p (c k) -> p c k", c=nchA)
                    DVE.tensor_tensor(brv, t1v[:, :, 0, :], t2v[:, :, 1, :], ALU.subtract)
                    tpool = S["dve"].inc(DVE.tensor_tensor(biv, t2v[:, :, 0, :], t1v[:, :, 1, :], ALU.add))
                    tk["ftmp_free"] = tpool
                i = psD_next()
                W(PE, tk["psD%d" % i])
                W(PE, tpool)
                PE.matmul(psD[i][:], c2s2[:, 0:128], bri[:, bset, 0, :], start=True, stop=False)
                ins = PE.matmul(psD[i][:], c2s2[:, 128:256], bri[:, bset, 1, :], start=False, stop=True)
                tpc = S["pe"].inc(ins)
                tk["bri%d" % bset] = tpc
                W(ACT, tpc)
                ins = ACT.copy(mxv[:, :, c0:c0 + nchC].rearrange("p k c -> p c k"), psD[i][:].rearrange("p (c k) -> p c k", c=nchC))
                tk["psD%d" % i] = S["act"].inc(ins)
            tk["fft_pz_free"] = tp
            W(SY, tk["psD%d" % i])
            t = S["fst"].dma(SY.dma_start(out=MX[cb, off:off + Sq, :].rearrange("(k2 k1) c -> k2 k1 c", k1=N1), in_=mxv))
            tk["fft_mx_free"] = t
            return t

        def fourier_b_group(L, q, g0, dyn=False):
            j = L // 2
            xin, off, Sq, N1, yout = seqs[q]
            load_x(L, q, g0, dyn)
            plan_out_proj(wb_fout[j], ("mo", L))
            plan_mlp(L)
            W(PE, tk["act2_free"])
            W(ACT, tk["act2_free"])
            lastE = None
            bufs = [hn, junk]
            bfree = [tk["hn_free"], tk["junk_free"]]
            ld = [None] * 4

            def issue(s):
                r0 = off + g0 + s * 128
                if dyn:
                    srcm = MXT[:, g0 + s * 128:g0 + (s + 1) * 128, :].rearrange("cb t c -> t cb c")
                else:
                    srcm = MX[:, r0:r0 + 128, :].rearrange("cb t c -> t cb c")
                W(SY, bfree[s % 2])
                ld[s] = S["hld"].dma(SY.dma_start(out=bufs[s % 2][:, :].rearrange("p (cb c) -> p cb c", cb=32), in_=srcm))

            issue(0)
            for s in range(4):
                if s + 1 < 4:
                    issue(s + 1)
                W(PE, ld[s])
                tp, lastE = transposes(bufs[s % 2], s, act2, s * 128, 128)
                bfree[s % 2] = tp
            tk["hn_free"] = bfree[0]
            tk["junk_free"] = bfree[1]
            W(PE, lastE)
            W(DVE, tk["x_ld"])
            tp, last = out_proj(act2, ("mo", L), wb_fout[j])
            tk["act2_free"] = tp
            return last

        tk["fft_pz_free"] = None
        tk["fft_mx_free"] = None

        def select_tail(L):
            def sel_pass(ntiles, cand, dst, stage, accs, free_toks):
                st_free = free_toks
                acc_free = [None] * len(accs)
                tlast = None
                for i in range(ntiles):
                    acc = accs[i % len(accs)]
                    for b in range(NSPLIT):
                        for tkn in st_free:
                            W(SY, tkn)
                        tl = S["fld"].dma(SY.dma_start(out=stage, in_=cand(i, b)))
                        W(DVE, tl)
                        if b == 0:
                            W(DVE, acc_free[i % len(accs)])
                            tv = S["dve"].inc(DVE.tensor_scalar(acc, stage, selt[:, 0:1], None, op0=ALU.mult))
                        else:
                            W(DVE, tv)
                            tv = S["dve"].inc(DVE.scalar_tensor_tensor(acc, stage, selt[:, b:b + 1], acc, op0=ALU.mult, op1=ALU.add))
                        st_free = [tv]
                    W(SY, tv)
                    tlast = S["fst"].dma(SY.dma_start(out=dst(i), in_=acc))
                    acc_free[i % len(accs)] = tlast
                return tlast, tv

            src = R[L % 2]
            for tkn in (tk["x_free"], tk["io_free"], tk["hn_free"]):
                W(DVE, tkn)
            t1, tv1 = sel_pass(BLK // 128,
                               lambda i, b: src[b * BLK + i * 128:b * BLK + (i + 1) * 128, :],
                               lambda i: RT[i * 128:(i + 1) * 128, :],
                               io_f32, [x[:, k, :] for k in range(4)], [tk["io_free"], tk["x_free"]])
            t2, tv2 = sel_pass(32,
                               lambda i, b: MX[i, b * BLK:(b + 1) * BLK, :].rearrange("(p r) c -> p (r c)", p=128),
                               lambda i: MXT[i].rearrange("(p r) c -> p (r c)", p=128),
                               junk[:, 0:(BLK // 128) * 64], [hn[:, 0:(BLK // 128) * 64], hn[:, 1024:1024 + (BLK // 128) * 64]], [tk["hn_free"]])
            W(SY, t1)
            W(SY, t2)
            tk["x_free"] = t1
            tk["io_free"] = tv1
            tk["hn_free"] = t2

        def drain_ring_for_fft():
            n = state["used"]
            for i in range(max(0, n - NSLOT), n):
                W(SY, state["free_tok"][i])

        groups = [(q, g0) for q in range(len(seqs)) for g0 in range(0, seqs[q][2], T)]
        for L in range(DEPTH):
            if L % 2 == 0:
                for (q, g0) in groups:
                    mix_done = conv_group(L, q, g0)
                    ffn_part(L, q, g0, mix_done)
            else:
                SY.wait_ge(S["iost"].h, S["iost"].n)
                drain_ring_for_fft()
                tlast = None
                for q in range(len(seqs)):
                    for cb in range(32):
                        tlast = fft_unit(q, cb)
                W(SY, tlast)
                W(PE, tlast)
                if L == DEPTH - 1 and NSPLIT > 1:
                    gl = [(0, g0, True) for g0 in range(0, BLK, T)] + [(q, g0, False) for (q, g0) in groups if q > 0]
                    select_tail(L)
                else:
                    gl = [(q, g0, False) for (q, g0) in groups]
                for (q, g0, dyn) in gl:
                    mix_done = fourier_b_group(L, q, g0, dyn)
                    ffn_part(L, q, g0, mix_done, dyn)
        W(SY, tk["x_free"])
        for nm in ("iost", "xst", "fst"):
            if S[nm].n:
                SY.wait_ge(S[nm].h, S[nm].n)
    return nc


def _consts(SP, SS):
    def ra(n1):
        n = np.arange(n1)
        a = 2 * np.pi * np.outer(n, n) / n1
        c, s = np.cos(a), np.sin(a)
        return (np.concatenate([c, -s, -s, -c], 1) / np.sqrt(n1)).astype(np.float32)

    def tw(n1):
        N = n1 * 128
        a = 2 * np.pi * np.outer(np.arange(128), np.arange(n1)) / N
        return np.concatenate([np.cos(a), -np.sin(a)], 1).astype(np.float32)

    a = 2 * np.pi * np.outer(np.arange(256), np.arange(256)) / 256
    ccsc = (np.concatenate([np.cos(a), np.sin(a)], 1) / 16.0).astype(np.float32)
    a = 2 * np.pi * np.outer(np.arange(128), np.arange(128)) / 128
    c2s2 = (np.concatenate([np.cos(a), np.sin(a)], 1) / np.sqrt(128.0)).astype(np.float32)
    return {"c_ident": np.eye(128, dtype=np.float32), "c_ccsc": ccsc, "c_ra_p": ra(SP // 128), "c_ra_s": ra(SS // 128),
            "c_tw_p": tw(SP // 128), "c_tw_s": tw(SS // 128), "c_c2s2": c2s2}


def run(inputs, SP, SS, NS, DFF, DEPTH, n_cores):
    nc = build(SP, SS, NS, DFF, DEPTH, n_cores)
    consts = _consts(SP, SS)
    xs = np.ascontiguousarray(inputs["x_sample"]).reshape(-1, D)
    in_maps = []
    for c in range(n_cores):
        m = {k: np.ascontiguousarray(inputs[k]) for k in
             ["norm_mix", "a_w_in", "a_conv_w", "a_w_out", "f_w_out", "norm_ffn", "w_up", "w_down"]}
        m["final_norm"] = np.ascontiguousarray(inputs["final_norm"]).reshape(1, D)
        m["x_prompt"] = np.ascontiguousarray(inputs["x_prompt"]).reshape(SP, D)
        m["x_sample"] = xs[c * NS * SS:(c + 1) * NS * SS]
        m.update(consts)
        sel = np.zeros((128, n_cores), np.float32)
        sel[:, c] = 1.0
        m["c_sel"] = sel
        in_maps.append(m)
    res = run_bass_kernel_spmd(nc, in_maps, core_ids=list(range(n_cores)))
    blk = SP // n_cores
    yp = np.concatenate([res.results[c]["y_prompt"] for c in range(n_cores)], 0).reshape(1, SP, D)
    ys = np.concatenate([res.results[c]["y_sample"] for c in range(n_cores)], 0).reshape(n_cores * NS, SS, D)
    return yp.astype(np.float32), ys.astype(np.float32)


def kernel(x_prompt, x_sample, norm_mix, a_w_in, a_conv_w, a_w_out, f_w_out, norm_ffn, w_up, w_down, final_norm):
    inputs = dict(x_prompt=x_prompt, x_sample=x_sample, norm_mix=norm_mix, a_w_in=a_w_in, a_conv_w=a_conv_w,
                  a_w_out=a_w_out, f_w_out=f_w_out, norm_ffn=norm_ffn, w_up=w_up, w_down=w_down, final_norm=final_norm)
    inputs = {k: np.asarray(v) for k, v in inputs.items()}
    return run(inputs, SP=16384, SS=2048, NS=2, DFF=8192, DEPTH=4, n_cores=8)
```

```python
import numpy as np
from contextlib import ExitStack

import concourse.bass as bass
import concourse.mybir as mybir
from concourse.bass_utils import run_bass_kernel_spmd

F32 = mybir.dt.float32
BF16 = mybir.dt.bfloat16
ALU = mybir.AluOpType
AF = mybir.ActivationFunctionType

D = 2048
KC = 16
T = 512
NSLOT = 4
SLOT = 8192
EPS = 1e-6


class Sem:
    def __init__(self, nc, es, name):
        self.h = es.enter_context(nc.semaphore(name))
        self.n = 0

    def inc(self, ins, k=1):
        ins.then_inc(self.h, k)
        self.n += k
        return (self, self.n)

    def dma(self, ins):
        return self.inc(ins, 16)


def W(eng, tok):
    if tok is not None:
        eng.wait_ge(tok[0].h, tok[1])


def build(SP, SS, NS, DFF, DEPTH, NSPLIT, TAIL_SPLIT=True):
    if not TAIL_SPLIT:
        NSPLIT = 1
    nc = bass.Bass("TRN2", target_bir_lowering=False)
    NA = (DEPTH + 1) // 2
    NB = max(DEPTH // 2, 1)
    NT = SP + NS * SS
    NFB = DFF // 512
    N1P, N1S = SP // 128, SS // 128
    BLK = SP // NSPLIT

    def din(name, shape, dt=F32):
        return nc.dram_tensor(name, list(shape), dt, kind="ExternalInput").ap()

    x_prompt = din("x_prompt", [SP, D])
    x_sample = din("x_sample", [NS * SS, D])
    norm_mix = din("norm_mix", [DEPTH, D])
    a_w_in = din("a_w_in", [NA, D, 3 * D])
    a_conv_w = din("a_conv_w", [NA, 3, D])
    a_w_out = din("a_w_out", [NA, D, D])
    f_w_out = din("f_w_out", [NB, D, D])
    norm_ffn = din("norm_ffn", [DEPTH, D])
    w_up = din("w_up", [DEPTH, D, DFF])
    w_down = din("w_down", [DEPTH, DFF, D])
    final_norm = din("final_norm", [1, D])
    c_ident = din("c_ident", [128, 128])
    c_ccsc = din("c_ccsc", [256, 512])
    c_ra_p = din("c_ra_p", [N1P, 4 * N1P])
    c_ra_s = din("c_ra_s", [N1S, 4 * N1S])
    c_tw_p = din("c_tw_p", [128, 2 * N1P])
    c_tw_s = din("c_tw_s", [128, 2 * N1S])
    c_c2s2 = din("c_c2s2", [128, 256])
    c_sel = din("c_sel", [128, max(NSPLIT, 1)])
    y_prompt = nc.dram_tensor("y_prompt", [SP // NSPLIT, D], F32, kind="ExternalOutput").ap()
    y_sample = nc.dram_tensor("y_sample", [NS * SS, D], F32, kind="ExternalOutput").ap()

    R = [nc.dram_tensor("res%d" % i, [NT, D], F32).ap() for i in range(2)]
    PQ = nc.dram_tensor("pq", [32, NT, 128], BF16).ap()
    MX = nc.dram_tensor("mx", [32, NT, 64], BF16).ap()
    MXT = nc.dram_tensor("mxt", [32, SP // NSPLIT, 64], BF16).ap()
    RT = nc.dram_tensor("rtail", [SP // NSPLIT, D], F32).ap()
    wb_in = nc.dram_tensor("wb_in", [NA, 16, D, 384], BF16).ap()
    wb_aout = nc.dram_tensor("wb_aout", [NA, D, D], BF16).ap()
    wb_fout = nc.dram_tensor("wb_fout", [NB, D, D], BF16).ap()
    wb_up = nc.dram_tensor("wb_up", [DEPTH, D, DFF], BF16).ap()
    wb_down = nc.dram_tensor("wb_down", [DEPTH, DFF, D], BF16).ap()

    seqs = [(x_prompt, 0, SP, N1P, y_prompt)]
    for i in range(NS):
        seqs.append((x_sample[i * SS:(i + 1) * SS, :], SP + i * SS, SS, N1S, y_sample[i * SS:(i + 1) * SS, :]))

    es = ExitStack()
    with es:
        def sb(name, shape, dt):
            return es.enter_context(nc.sbuf_tensor(name, list(shape), dt))

        def ps(name, shape, dt):
            return es.enter_context(nc.psum_tensor(name, list(shape), dt))

        x = sb("x", [128, 4, D], F32)
        gb = sb("gb", [128, D], F32)
        junk = sb("junk", [128, D], BF16)
        hn = sb("hn", [128, D], BF16)
        hT = sb("hT", [128, KC, T + 2], BF16)
        act2 = sb("act2", [128, KC, T], BF16)
        hid = sb("hid", [128, 2, 4, T], BF16)
        tmpr = sb("tmpr", [128, 2, T], F32)
        ftmp = sb("ftmp", [128, 4, T + 2], F32)
        ring = sb("ring", [128, NSLOT * SLOT], BF16)
        io = sb("io", [128, 2 * D], BF16)
        bri = sb("bri", [128, 2, 2, 512], BF16)
        ss = sb("ss", [128, 32], F32)
        idb = sb("idb", [128, 128], BF16)
        ccsc = sb("ccsc", [128, 2, 512], BF16)
        ra_p = sb("ra_p", [128, 4 * N1P], BF16)
        ra_s = sb("ra_s", [128, 4 * N1S], BF16)
        tw_p = sb("tw_p", [128, 2 * N1P], F32)
        tw_s = sb("tw_s", [128, 2 * N1S], F32)
        c2s2 = sb("c2s2", [128, 256], BF16)
        cwt = sb("cwt", [128, NA * 3 * KC], F32)
        selt = sb("selt", [128, max(NSPLIT, 1)], F32)

        psU = [ps("psU%d" % i, [128, 512], F32) for i in range(2)]
        psD = [ps("psD%d" % i, [128, 512], F32) for i in range(3)]
        psX = ps("psX", [128, 512], F32)
        psT = [ps("psT%d" % i, [128, 8, 128], BF16) for i in range(2)]

        io_f32 = io[:, :].bitcast(F32)
        xh = io_f32

        S = {n: Sem(nc, es, n) for n in
             ["cast", "cst", "xld", "xst", "gbl", "pe", "act", "dve", "pool", "hld", "iost", "fld", "fst"]}
        ring_ld = [Sem(nc, es, "rl%d" % i) for i in range(NSLOT)]
        PE, ACT, DVE, POOL, SY = nc.tensor, nc.scalar, nc.vector, nc.gpsimd, nc.sync
        dynv = {}

        def prow(ap, r0, n):
            return ap.rearrange("(b t) d -> b t d", t=BLK)[bass.ds(dynv["pid"], 1), r0:r0 + n, :]

        POOL.dma_start(out=idb[:], in_=c_ident[:, :]).then_inc(S["cst"].h, 16)
        POOL.dma_start(out=ccsc[:], in_=c_ccsc.rearrange("(k p) c -> p k c", p=128)).then_inc(S["cst"].h, 16)
        POOL.dma_start(out=ra_p[0:N1P, :], in_=c_ra_p[:, :]).then_inc(S["cst"].h, 16)
        POOL.dma_start(out=ra_s[0:N1S, :], in_=c_ra_s[:, :]).then_inc(S["cst"].h, 16)
        POOL.dma_start(out=c2s2[:], in_=c_c2s2[:, :]).then_inc(S["cst"].h, 16)
        SY.dma_start(out=tw_p[:], in_=c_tw_p[:, :]).then_inc(S["cst"].h, 16)
        SY.dma_start(out=tw_s[:], in_=c_tw_s[:, :]).then_inc(S["cst"].h, 16)
        SY.dma_start(out=selt[:], in_=c_sel[:, :]).then_inc(S["cst"].h, 16)
        for j in range(NA):
            for t in range(3):
                o = (j * 3 + t) * KC
                SY.dma_start(out=cwt[:, o:o + KC], in_=a_conv_w[j, t:t + 1, :].rearrange("o (c p) -> p (o c)", p=128),
                             allow_slow_non_contiguous=True).then_inc(S["cst"].h, 16)
        S["cst"].n = 16 * (8 + 3 * NA)
        cst_tok = (S["cst"], S["cst"].n)
        for e in (PE, ACT, DVE, POOL):
            W(e, cst_tok)

        cast_tok = {}

        def cast2d(key, dst, src, rows, cols):
            sem = Sem(nc, es, "c_%s%d" % key)
            for r0 in range(0, rows, 128):
                for c0 in range(0, cols, 2048):
                    c1 = min(cols, c0 + 2048)
                    t = sem.dma(POOL.dma_start(out=dst[r0:r0 + 128, c0:c1], in_=src[r0:r0 + 128, c0:c1]))
            cast_tok[key] = t

        for L in range(DEPTH):
            j = L // 2
            if L % 2 == 0:
                sem_in = Sem(nc, es, "c_in%d" % L)
                for cc in range(KC):
                    for r0 in range(0, D, 128):
                        t = sem_in.dma(POOL.dma_start(
                            out=wb_in[j, cc, r0:r0 + 128, :].rearrange("k (g c) -> k g c", g=3),
                            in_=a_w_in[j, r0:r0 + 128, :].rearrange("k (g c) -> k g c", g=3)[:, :, cc * 128:(cc + 1) * 128]))
                cast_tok[("in", L)] = t
                cast2d(("mo", L), wb_aout[j], a_w_out[j], D, D)
            else:
                cast2d(("mo", L), wb_fout[j], f_w_out[j], D, D)
            cast2d(("up", L), wb_up[L], w_up[L], D, DFF)
            cast2d(("dn", L), wb_down[L], w_down[L], DFF, D)

        plan = []
        state = {"issued": 0, "used": 0, "free_tok": {}, "cast_waited": set()}

        def ring_plan(ap, cast_key):
            plan.append((ap, cast_key))

        def ring_issue_upto(b):
            while state["issued"] <= min(b, len(plan) - 1):
                i = state["issued"]
                ap, ck = plan[i]
                slot = i % NSLOT
                if ck not in state["cast_waited"]:
                    W(SY, cast_tok[ck])
                    state["cast_waited"].add(ck)
                if i >= NSLOT:
                    W(SY, state["free_tok"][i - NSLOT])
                shp = ap.shape
                n = shp[1] * shp[2]
                dst = ring[:, slot * SLOT: slot * SLOT + n].rearrange("p (a b) -> p a b", a=shp[1])
                ring_ld[slot].dma(SY.dma_start(out=dst, in_=ap))
                state["issued"] += 1

        def ring_next():
            b = state["used"]
            ring_issue_upto(b + NSLOT - 1)
            slot = b % NSLOT
            ap, _ = plan[b]
            shp = ap.shape
            n = shp[1] * shp[2]
            view = ring[:, slot * SLOT: slot * SLOT + n].rearrange("p (a b) -> p a b", a=shp[1])
            PE.wait_ge(ring_ld[slot].h, 16 * (b // NSLOT + 1))
            state["used"] += 1
            return b, view

        def ring_free(b, tok):
            state["free_tok"][b] = tok

        tk = {k: None for k in ["x_free", "x_ld", "gb_ld", "gb_free", "hn_free", "psT0", "psT1", "hT_free", "act2_free",
                                 "io_free", "psU0", "psU1", "psD0", "psD1", "psD2", "psX", "tmpr0", "tmpr1",
                                 "hid0", "hid1", "ftmp_free", "bg_free", "u_free", "hnld_free", "bri0", "bri1", "xh_ld", "junk_free"]}
        cnt = {"psD": 0, "psU": 0}

        def load_gb(vec_ap):
            W(SY, tk["gb_free"])
            tk["gb_ld"] = S["gbl"].dma(SY.dma_start(out=gb[:], in_=vec_ap.partition_broadcast(128)))

        def transposes(src_tile, s, dst, ncols, width=128):
            last = None
            for half in range(2):
                W(PE, tk["psT%d" % half])
                for c in range(8):
                    ins = PE.transpose(psT[half][:, c, :], src_tile[:, (half * 8 + c) * 128:(half * 8 + c + 1) * 128], idb[:])
                tp = S["pe"].inc(ins)
                W(ACT, tp)
                ins = ACT.copy(dst[:, half * 8:half * 8 + 8, ncols:ncols + width], psT[half][:, :, 0:width])
                tk["psT%d" % half] = S["act"].inc(ins)
                last = tp
            return last, tk["psT1"]

        def norm_stats(nsub, srcs):
            t0 = S["dve"].inc(DVE.memset(ss[:, 8:8 + nsub], 0.0))
            W(ACT, t0)
            W(ACT, tk["x_ld"])
            W(ACT, tk["junk_free"])
            for s in range(nsub):
                ins = ACT.activation(junk[:], srcs[s], AF.Square, accum_out=ss[:, 8 + s:9 + s])
            ta = S["act"].inc(ins)
            W(DVE, ta)
            tv = S["dve"].inc(DVE.tensor_scalar(ss[:, 16:16 + nsub], ss[:, 8:8 + nsub], 1.0 / D, EPS, op0=ALU.mult, op1=ALU.add))
            W(ACT, tv)
            ta = S["act"].inc(ACT.sqrt(ss[:, 24:24 + nsub], ss[:, 16:16 + nsub]))
            W(DVE, ta)
            tv = S["dve"].inc(DVE.reciprocal(ss[:, 0:nsub], ss[:, 24:24 + nsub]))
            W(DVE, tv)
            return tv

        def norm_to_hT(halo):
            nsub = 5 if halo else 4
            srcs = [x[:, s, :] for s in range(4)] + ([xh] if halo else [])
            if halo:
                W(ACT, tk["xh_ld"])
            norm_stats(nsub, srcs)
            W(DVE, tk["gb_ld"])
            W(PE, tk["hT_free"])
            W(ACT, tk["hT_free"])
            lastE = None
            for s in range(nsub):
                W(DVE, tk["hn_free"])
                th = S["dve"].inc(DVE.scalar_tensor_tensor(hn[:], srcs[s], ss[:, s:s + 1], gb[:], op0=ALU.mult, op1=ALU.mult))
                W(PE, th)
                if s < 4:
                    tp, lastE = transposes(hn, s, hT, s * 128, 128)
                else:
                    tp, lastE = transposes(hn, s, hT, T, 2)
                tk["hn_free"] = tp
            tk["gb_free"] = th
            return lastE

        def psD_next():
            i = cnt["psD"] % 3
            cnt["psD"] += 1
            return i

        def out_proj(src, wkey, wdram):
            last = None
            for db in range(4):
                b, wv = ring_next()
                for s in range(4):
                    i = psD_next()
                    W(PE, tk["psD%d" % i])
                    for cc in range(KC):
                        ins = PE.matmul(psD[i][:], src[:, cc, s * 128:(s + 1) * 128], wv[:, cc, :], start=(cc == 0), stop=(cc == KC - 1))
                    tp = S["pe"].inc(ins)
                    W(DVE, tp)
                    ins = DVE.tensor_tensor(x[:, s, db * 512:(db + 1) * 512], psD[i][:], x[:, s, db * 512:(db + 1) * 512], ALU.add)
                    tk["psD%d" % i] = S["dve"].inc(ins)
                    last = tk["psD%d" % i]
                ring_free(b, tp)
            return tp, last

        def plan_out_proj(wdram, key):
            for db in range(4):
                ring_plan(wdram.rearrange("(cc p) d -> p cc d", p=128)[:, :, db * 512:(db + 1) * 512], key)

        def plan_mlp(L):
            up = lambda fb: ring_plan(wb_up[L].rearrange("(kc p) f -> p kc f", p=128)[:, :, fb * 512:(fb + 1) * 512], ("up", L))
            dn = lambda fb: ring_plan(wb_down[L][fb * 512:(fb + 1) * 512, :].rearrange("(fc p) d -> p fc d", p=128), ("dn", L))
            up(0)
            for fb in range(NFB):
                if fb + 1 < NFB:
                    up(fb + 1)
                dn(fb)

        def mlp(L, hT_ready):
            W(PE, hT_ready)
            res = {"last_add": None, "tp": None}

            def up_block(fb):
                hb = fb % 2
                bu, wu = ring_next()
                tsq = None
                for fc in range(4):
                    ui = cnt["psU"] % 2
                    cnt["psU"] += 1
                    W(PE, tk["psU%d" % ui])
                    for kc in range(KC):
                        ins = PE.matmul(psU[ui][:], wu[:, kc, fc * 128:(fc + 1) * 128], hT[:, kc, 0:T], start=(kc == 0), stop=(kc == KC - 1))
                    tp = S["pe"].inc(ins)
                    W(ACT, tp)
                    W(ACT, tk["tmpr%d" % ui])
                    ta = S["act"].inc(ACT.activation(tmpr[:, ui, :], psU[ui][:], AF.Relu))
                    tk["psU%d" % ui] = ta
                    W(DVE, ta)
                    if fc == 0:
                        W(DVE, tk["hid%d" % hb])
                    tsq = S["dve"].inc(DVE.tensor_tensor(hid[:, hb, fc, :], tmpr[:, ui, :], tmpr[:, ui, :], ALU.mult))
                    tk["tmpr%d" % ui] = tsq
                ring_free(bu, tp)
                res["tp_up"] = tp
                return tsq

            def down_block(fb, tsq):
                hb = fb % 2
                bd, wd = ring_next()
                W(PE, tsq)
                tp = None
                for s in range(4):
                    for db in range(4):
                        i = psD_next()
                        W(PE, tk["psD%d" % i])
                        for fc in range(4):
                            ins = PE.matmul(psD[i][:], hid[:, hb, fc, s * 128:(s + 1) * 128], wd[:, fc, db * 512:(db + 1) * 512],
                                            start=(fc == 0), stop=(fc == 3))
                        tp = S["pe"].inc(ins)
                        W(DVE, tp)
                        ins = DVE.tensor_tensor(x[:, s, db * 512:(db + 1) * 512], psD[i][:], x[:, s, db * 512:(db + 1) * 512], ALU.add)
                        tk["psD%d" % i] = S["dve"].inc(ins)
                        res["last_add"] = tk["psD%d" % i]
                ring_free(bd, tp)
                tk["hid%d" % hb] = tp
                res["tp"] = tp

            tsqs = {0: up_block(0)}
            for fb in range(NFB):
                if fb + 1 < NFB:
                    tsqs[fb + 1] = up_block(fb + 1)
                down_block(fb, tsqs[fb])
            tk["hT_free"] = res["tp"]
            return res["last_add"]

        def src_rows(L, q, r0, n):
            xin, off, Sq, N1, yout = seqs[q]
            if L == 0:
                return xin[r0:r0 + n, :]
            return R[L % 2][off + r0:off + r0 + n, :]

        def dst_rows(L, q, r0, n):
            xin, off, Sq, N1, yout = seqs[q]
            return R[(L + 1) % 2][off + r0:off + r0 + n, :]

        def load_x(L, q, g0, dyn=False):
            W(SY, tk["x_free"])
            for s in range(4):
                src = RT[g0 + s * 128:g0 + (s + 1) * 128, :] if dyn else src_rows(L, q, g0 + s * 128, 128)
                t = S["xld"].dma(SY.dma_start(out=x[:, s, :], in_=src))
            tk["x_ld"] = t

        def store_x(L, q, g0, done_tok):
            W(SY, done_tok)
            for s in range(4):
                t = S["xst"].dma(SY.dma_start(out=dst_rows(L, q, g0 + s * 128, 128), in_=x[:, s, :]))
            W(SY, t)
            tk["x_free"] = t

        def final_out(q, g0, done_tok, dyn=False):
            xin, off, Sq, N1, yout = seqs[q]
            load_gb(final_norm[0:1, :])
            W(ACT, done_tok)
            W(DVE, done_tok)
            norm_stats(4, [x[:, s, :] for s in range(4)])
            W(DVE, tk["gb_ld"])
            t = None
            for s in range(4):
                W(DVE, tk["io_free"])
                tv = S["dve"].inc(DVE.scalar_tensor_tensor(io_f32, x[:, s, :], ss[:, s:s + 1], gb[:], op0=ALU.mult, op1=ALU.mult))
                W(SY, tv)
                dsta = yout[g0 + s * 128:g0 + (s + 1) * 128, :]
                t = S["iost"].dma(SY.dma_start(out=dsta, in_=io_f32))
                tk["io_free"] = t
            tk["gb_free"] = tv
            W(SY, t)
            tk["x_free"] = t

        def conv_group(L, q, g0):
            j = L // 2
            xin, off, Sq, N1, yout = seqs[q]
            load_x(L, q, g0)
            W(DVE, tk["io_free"])
            tz = S["dve"].inc(DVE.memset(xh[:, :], 0.0))
            W(SY, tz)
            t = tz
            if g0 > 0:
                t = S["hld"].dma(SY.dma_start(out=xh[0:1, :], in_=src_rows(L, q, g0 - 1, 1)))
            if g0 + T < Sq:
                t = S["hld"].dma(SY.dma_start(out=xh[1:2, :], in_=src_rows(L, q, g0 + T, 1)))
            tk["xh_ld"] = t
            load_gb(norm_mix[L:L + 1, :])
            for cc in range(KC):
                ring_plan(wb_in[j, cc].rearrange("(kc p) c -> p kc c", p=128), ("in", L))
            plan_out_proj(wb_aout[j], ("mo", L))
            plan_mlp(L)
            hT_ready = norm_to_hT(True)
            tk["io_free"] = hT_ready
            W(PE, hT_ready)
            Bg, Cg, u, acc = ftmp[:, 0, 0:T], ftmp[:, 1, :], ftmp[:, 2, :], ftmp[:, 3, 0:T]
            psB, psC, psV, psH = psU[0], psU[1], psD[0], psX
            W(PE, tk["psD0"]); W(PE, tk["psU0"]); W(PE, tk["psU1"])
            W(DVE, tk["act2_free"])
            tpool = None
            tAB = tACm = tu = t1 = None
            hTm, hTh = hT[:, :, 0:T], hT[:, :, T:T + 2]

            def mm16(pt, wv, c0, rhs):
                for kc in range(KC):
                    ins = PE.matmul(pt, wv[:, kc, c0:c0 + 128], rhs[:, kc, :], start=(kc == 0), stop=(kc == KC - 1))
                return ins

            for cc in range(KC):
                b, wv = ring_next()
                W(PE, tAB)
                tB = S["pe"].inc(mm16(psB[:], wv, 0, hTm))
                W(PE, tACm)
                tC = S["pe"].inc(mm16(psC[:], wv, 128, hTm))
                W(PE, tu)
                mm16(psH[:, 0:2], wv, 128, hTh)
                mm16(psV[:], wv, 256, hTm)
                tp = S["pe"].inc(mm16(psH[:, 2:4], wv, 256, hTh))
                ring_free(b, tp)
                W(ACT, tB)
                W(ACT, tpool)
                tAB = S["act"].inc(ACT.copy(Bg, psB[:]))
                W(ACT, tC)
                W(ACT, tu)
                tACm = S["act"].inc(ACT.copy(Cg[:, 1:T + 1], psC[:]))
                W(ACT, tp)
                ACT.copy(Cg[:, 0:1], psH[:, 0:1])
                tACh = S["act"].inc(ACT.copy(Cg[:, T + 1:T + 2], psH[:, 1:2]))
                W(DVE, tACh)
                W(DVE, tpool)
                DVE.tensor_tensor(u[:, 1:T + 1], psV[:], Cg[:, 1:T + 1], ALU.mult)
                DVE.tensor_tensor(u[:, 0:1], psH[:, 2:3], Cg[:, 0:1], ALU.mult)
                tu = S["dve"].inc(DVE.tensor_tensor(u[:, T + 1:T + 2], psH[:, 3:4], Cg[:, T + 1:T + 2], ALU.mult))
                o = (j * 3) * KC + cc
                W(DVE, tu)
                t1 = S["dve"].inc(DVE.tensor_scalar(acc, u[:, 0:T], cwt[:, o:o + 1], None, op0=ALU.mult))
                W(DVE, t1)
                t1 = S["dve"].inc(DVE.scalar_tensor_tensor(acc, u[:, 1:T + 1], cwt[:, o + KC:o + KC + 1], acc, op0=ALU.mult, op1=ALU.add))
                W(DVE, t1)
                t1 = S["dve"].inc(DVE.scalar_tensor_tensor(acc, u[:, 2:T + 2], cwt[:, o + 2 * KC:o + 2 * KC + 1], acc, op0=ALU.mult, op1=ALU.add))
                W(DVE, t1)
                W(DVE, tAB)
                tpool = S["dve"].inc(DVE.tensor_tensor(act2[:, cc, :], acc, Bg, ALU.mult))
            tk["psU0"] = tAB; tk["psU1"] = tACm; tk["psD0"] = tu; tk["psX"] = tu
            tk["hT_free"] = tp
            W(PE, tpool)
            W(DVE, tk["x_ld"])
            tp, last = out_proj(act2, ("mo", L), wb_aout[j])
            tk["act2_free"] = tp
            return last

        def ffn_part(L, q, g0, mix_done, dyn=False):
            load_gb(norm_ffn[L:L + 1, :])
            W(ACT, mix_done)
            W(DVE, mix_done)
            hT_ready = norm_to_hT(False)
            last_add = mlp(L, hT_ready)
            if L == DEPTH - 1:
                final_out(q, g0, last_add, dyn)
            else:
                done = last_add
                if (L + 1) % 2 == 1:
                    done = fourier_a_body(L + 1, q, g0, last_add)
                store_x(L, q, g0, done)

        def fourier_a_body(L, q, g0, x_done):
            xin, off, Sq, N1, yout = seqs[q]
            load_gb(norm_mix[L:L + 1, :])
            W(ACT, x_done)
            W(DVE, x_done)
            hT_ready = norm_to_hT(False)
            x_read_done = tk["gb_free"]
            W(PE, hT_ready)
            iov = io[:, :].rearrange("p (cb h c) -> p cb h c", cb=32, h=2)
            tp = None
            for s in range(4):
                W(ACT, tk["io_free"])
                for g8 in range(8):
                    ui = cnt["psU"] % 2
                    cnt["psU"] += 1
                    W(PE, tk["psU%d" % ui])
                    for h in range(2):
                        for kk in range(2):
                            ins = PE.matmul(psU[ui][:, h * 256:(h + 1) * 256], hT[:, 2 * g8 + kk, s * 128:(s + 1) * 128],
                                            ccsc[:, kk, h * 256:(h + 1) * 256], start=(kk == 0), stop=(kk == 1))
                    tp = S["pe"].inc(ins)
                    W(ACT, tp)
                    ins = ACT.copy(iov[:, 4 * g8:4 * g8 + 4, :, :].rearrange("p cb h c -> p h cb c"),
                                   psU[ui][:].rearrange("p (h cb c) -> p h cb c", h=2, cb=4))
                    tk["psU%d" % ui] = S["act"].inc(ins)
                W(SY, tk["psU%d" % ui])
                r0 = off + g0 + s * 128
                t = S["iost"].dma(SY.dma_start(out=PQ[:, r0:r0 + 128, :].rearrange("cb t c -> t cb c"),
                                               in_=io[:, :].rearrange("p (cb c) -> p cb c", cb=32)))
                tk["io_free"] = t
            tk["hT_free"] = tp
            return x_read_done

        def fft_unit(q, cb):
            xin, off, Sq, N1, yout = seqs[q]
            ra = ra_p if N1 == N1P else ra_s
            tw = tw_p if N1 == N1P else tw_s
            if N1P == N1S:
                ra, tw = ra_p, tw_p
            nchA = 256 // N1
            nchC = 2 * nchA
            pz = ring[:, 0:2 * SLOT].rearrange("p (n h c) -> p n h c", n=128, h=2)
            mxv = ring[:, 2 * SLOT:2 * SLOT + N1 * 64].rearrange("p (k c) -> p k c", c=64)
            W(SY, tk["fft_pz_free"])
            t = S["fld"].dma(SY.dma_start(out=pz[0:N1], in_=PQ[cb, off:off + Sq, :].rearrange("(n1 n2) (h c) -> n1 n2 h c", n2=128, h=2)))
            W(PE, t)
            trb = tw[:, 0:N1].unsqueeze(1).unsqueeze(1).broadcast_to([128, nchA, 2, N1])
            tib = tw[:, N1:2 * N1].unsqueeze(1).unsqueeze(1).broadcast_to([128, nchA, 2, N1])
            t1v = ftmp[:, 0, 0:512].rearrange("p (c r k) -> p c r k", c=nchA, r=2)
            t2v = ftmp[:, 1, 0:512].rearrange("p (c r k) -> p c r k", c=nchA, r=2)
            W(ACT, tk["fft_mx_free"])
            tp = None
            for ci, c0 in enumerate(range(0, 64, nchC)):
                bset = ci % 2
                tpool = None
                for hh in range(2):
                    ui = cnt["psU"] % 2
                    cnt["psU"] += 1
                    W(PE, tk["psU%d" % ui])
                    for jc in range(nchA):
                        ch = c0 + hh * nchA + jc
                        PE.matmul(psU[ui][:, jc * 2 * N1:(jc + 1) * 2 * N1], pz[0:N1, :, 0, ch], ra[0:N1, 0:2 * N1], start=True, stop=False)
                        ins = PE.matmul(psU[ui][:, jc * 2 * N1:(jc + 1) * 2 * N1], pz[0:N1, :, 1, ch], ra[0:N1, 2 * N1:4 * N1], start=False, stop=True)
                    tp = S["pe"].inc(ins)
                    W(DVE, tp)
                    W(DVE, tpool if tpool is not None else tk["ftmp_free"])
                    av = psU[ui][:].rearrange("p (c r k) -> p c r k", c=nchA, r=2)
                    DVE.tensor_tensor(t1v, av, trb, ALU.mult)
                    tv = S["dve"].inc(DVE.tensor_tensor(t2v, av, tib, ALU.mult))
                    tk["psU%d" % ui] = tv
                    W(DVE, tv)
                    if hh == 0:
                        W(DVE, tk["bri%d" % bset])
                    brv = bri[:, bset, 0, hh * 256:(hh + 1) * 256].rearrange("p (c k) -> p c k", c=nchA)
                    biv = bri[:, bset, 1, hh * 256:(hh + 1) * 256].rearrange("p (c k) -> p c k", c=nchA)
                    DVE.tensor_tensor(brv, t1v[:, :, 0, :], t2v[:, :, 1, :], ALU.subtract)
                    tpool = S["dve"].inc(DVE.tensor_tensor(biv, t2v[:, :, 0, :], t1v[:, :, 1, :], ALU.add))
                    tk["ftmp_free"] = tpool
                i = psD_next()
                W(PE, tk["psD%d" % i])
                W(PE, tpool)
                PE.matmul(psD[i][:], c2s2[:, 0:128], bri[:, bset, 0, :], start=True, stop=False)
                ins = PE.matmul(psD[i][:], c2s2[:, 128:256], bri[:, bset, 1, :], start=False, stop=True)
                tpc = S["pe"].inc(ins)
                tk["bri%d" % bset] = tpc
                W(ACT, tpc)
                ins = ACT.copy(mxv[:, :, c0:c0 + nchC].rearrange("p k c -> p c k"), psD[i][:].rearrange("p (c k) -> p c k", c=nchC))
                tk["psD%d" % i] = S["act"].inc(ins)
            tk["fft_pz_free"] = tp
            W(SY, tk["psD%d" % i])
            t = S["fst"].dma(SY.dma_start(out=MX[cb, off:off + Sq, :].rearrange("(k2 k1) c -> k2 k1 c", k1=N1), in_=mxv))
            tk["fft_mx_free"] = t
            return t

        def fourier_b_group(L, q, g0, dyn=False):
            j = L // 2
            xin, off, Sq, N1, yout = seqs[q]
            load_x(L, q, g0, dyn)
            plan_out_proj(wb_fout[j], ("mo", L))
            plan_mlp(L)
            W(PE, tk["act2_free"])
            W(ACT, tk["act2_free"])
            lastE = None
            bufs = [hn, junk]
            bfree = [tk["hn_free"], tk["junk_free"]]
            ld = [None] * 4

            def issue(s):
                r0 = off + g0 + s * 128
                if dyn:
                    srcm = MXT[:, g0 + s * 128:g0 + (s + 1) * 128, :].rearrange("cb t c -> t cb c")
                else:
                    srcm = MX[:, r0:r0 + 128, :].rearrange("cb t c -> t cb c")
                W(SY, bfree[s % 2])
                ld[s] = S["hld"].dma(SY.dma_start(out=bufs[s % 2][:, :].rearrange("p (cb c) -> p cb c", cb=32), in_=srcm))

            issue(0)
            for s in range(4):
                if s + 1 < 4:
                    issue(s + 1)
                W(PE, ld[s])
                tp, lastE = transposes(bufs[s % 2], s, act2, s * 128, 128)
                bfree[s % 2] = tp
            tk["hn_free"] = bfree[0]
            tk["junk_free"] = bfree[1]
            W(PE, lastE)
            W(DVE, tk["x_ld"])
            tp, last = out_proj(act2, ("mo", L), wb_fout[j])
            tk["act2_free"] = tp
            return last

        tk["fft_pz_free"] = None
        tk["fft_mx_free"] = None

        def select_tail(L):
            def sel_pass(ntiles, cand, dst, stage, accs, free_toks):
                st_free = free_toks
                acc_free = [None] * len(accs)
                tlast = None
                for i in range(ntiles):
                    acc = accs[i % len(accs)]
                    for b in range(NSPLIT):
                        for tkn in st_free:
                            W(SY, tkn)
                        tl = S["fld"].dma(SY.dma_start(out=stage, in_=cand(i, b)))
                        W(DVE, tl)
                        if b == 0:
                            W(DVE, acc_free[i % len(accs)])
                            tv = S["dve"].inc(DVE.tensor_scalar(acc, stage, selt[:, 0:1], None, op0=ALU.mult))
                        else:
                            W(DVE, tv)
                            tv = S["dve"].inc(DVE.scalar_tensor_tensor(acc, stage, selt[:, b:b + 1], acc, op0=ALU.mult, op1=ALU.add))
                        st_free = [tv]
                    W(SY, tv)
                    tlast = S["fst"].dma(SY.dma_start(out=dst(i), in_=acc))
                    acc_free[i % len(accs)] = tlast
                return tlast, tv

            src = R[L % 2]
            for tkn in (tk["x_free"], tk["io_free"], tk["hn_free"]):
                W(DVE, tkn)
            t1, tv1 = sel_pass(BLK // 128,
                               lambda i, b: src[b * BLK + i * 128:b * BLK + (i + 1) * 128, :],
                               lambda i: RT[i * 128:(i + 1) * 128, :],
                               io_f32, [x[:, k, :] for k in range(4)], [tk["io_free"], tk["x_free"]])
            t2, tv2 = sel_pass(32,
                               lambda i, b: MX[i, b * BLK:(b + 1) * BLK, :].rearrange("(p r) c -> p (r c)", p=128),
                               lambda i: MXT[i].rearrange("(p r) c -> p (r c)", p=128),
                               junk[:, 0:(BLK // 128) * 64], [hn[:, 0:(BLK // 128) * 64], hn[:, 1024:1024 + (BLK // 128) * 64]], [tk["hn_free"]])
            W(SY, t1)
            W(SY, t2)
            tk["x_free"] = t1
            tk["io_free"] = tv1
            tk["hn_free"] = t2

        def drain_ring_for_fft():
            n = state["used"]
            for i in range(max(0, n - NSLOT), n):
                W(SY, state["free_tok"][i])

        groups = [(q, g0) for q in range(len(seqs)) for g0 in range(0, seqs[q][2], T)]
        for L in range(DEPTH):
            if L % 2 == 0:
                for (q, g0) in groups:
                    mix_done = conv_group(L, q, g0)
                    ffn_part(L, q, g0, mix_done)
            else:
                SY.wait_ge(S["iost"].h, S["iost"].n)
                drain_ring_for_fft()
                tlast = None
                for q in range(len(seqs)):
                    for cb in range(32):
                        tlast = fft_unit(q, cb)
                W(SY, tlast)
                W(PE, tlast)
                if L == DEPTH - 1 and NSPLIT > 1:
                    gl = [(0, g0, True) for g0 in range(0, BLK, T)] + [(q, g0, False) for (q, g0) in groups if q > 0]
                    select_tail(L)
                else:
                    gl = [(q, g0, False) for (q, g0) in groups]
                for (q, g0, dyn) in gl:
                    mix_done = fourier_b_group(L, q, g0, dyn)
                    ffn_part(L, q, g0, mix_done, dyn)
        W(SY, tk["x_free"])
        for nm in ("iost", "xst", "fst"):
            if S[nm].n:
                SY.wait_ge(S[nm].h, S[nm].n)
    return nc


def _consts(SP, SS):
    def ra(n1):
        n = np.arange(n1)
        a = 2 * np.pi * np.outer(n, n) / n1
        c, s = np.cos(a), np.sin(a)
        return (np.concatenate([c, -s, -s, -c], 1) / np.sqrt(n1)).astype(np.float32)

    def tw(n1):
        N = n1 * 128
        a = 2 * np.pi * np.outer(np.arange(128), np.arange(n1)) / N
        return np.concatenate([np.cos(a), -np.sin(a)], 1).astype(np.float32)

    a = 2 * np.pi * np.outer(np.arange(256), np.arange(256)) / 256
    ccsc = (np.concatenate([np.cos(a), np.sin(a)], 1) / 16.0).astype(np.float32)
    a = 2 * np.pi * np.outer(np.arange(128), np.arange(128)) / 128
    c2s2 = (np.concatenate([np.cos(a), np.sin(a)], 1) / np.sqrt(128.0)).astype(np.float32)
    return {"c_ident": np.eye(128, dtype=np.float32), "c_ccsc": ccsc, "c_ra_p": ra(SP // 128), "c_ra_s": ra(SS // 128),
            "c_tw_p": tw(SP // 128), "c_tw_s": tw(SS // 128), "c_c2s2": c2s2}


def run(inputs, SP, SS, NS, DFF, DEPTH, n_cores):
    nc = build(SP, SS, NS, DFF, DEPTH, n_cores)
    consts = _consts(SP, SS)
    xs = np.ascontiguousarray(inputs["x_sample"]).reshape(-1, D)
    in_maps = []
    for c in range(n_cores):
        m = {k: np.ascontiguousarray(inputs[k]) for k in
             ["norm_mix", "a_w_in", "a_conv_w", "a_w_out", "f_w_out", "norm_ffn", "w_up", "w_down"]}
        m["final_norm"] = np.ascontiguousarray(inputs["final_norm"]).reshape(1, D)
        m["x_prompt"] = np.ascontiguousarray(inputs["x_prompt"]).reshape(SP, D)
        m["x_sample"] = xs[c * NS * SS:(c + 1) * NS * SS]
        m.update(consts)
        sel = np.zeros((128, n_cores), np.float32)
        sel[:, c] = 1.0
        m["c_sel"] = sel
        in_maps.append(m)
    res = run_bass_kernel_spmd(nc, in_maps, core_ids=list(range(n_cores)))
    blk = SP // n_cores
    yp = np.concatenate([res.results[c]["y_prompt"] for c in range(n_cores)], 0).reshape(1, SP, D)
    ys = np.concatenate([res.results[c]["y_sample"] for c in range(n_cores)], 0).reshape(n_cores * NS, SS, D)
    return yp.astype(np.float32), ys.astype(np.float32)


def kernel(x_prompt, x_sample, norm_mix, a_w_in, a_conv_w, a_w_out, f_w_out, norm_ffn, w_up, w_down, final_norm):
    inputs = dict(x_prompt=x_prompt, x_sample=x_sample, norm_mix=norm_mix, a_w_in=a_w_in, a_conv_w=a_conv_w,
                  a_w_out=a_w_out, f_w_out=f_w_out, norm_ffn=norm_ffn, w_up=w_up, w_down=w_down, final_norm=final_norm)
    inputs = {k: np.asarray(v) for k, v in inputs.items()}
    return run(inputs, SP=16384, SS=2048, NS=2, DFF=8192, DEPTH=4, n_cores=8)
```

```python
import numpy as np
from contextlib import ExitStack

import concourse.bass as bass
import concourse.mybir as mybir
from concourse.bass_utils import run_bass_kernel_spmd

F32 = mybir.dt.float32
BF16 = mybir.dt.bfloat16
ALU = mybir.AluOpType
AF = mybir.ActivationFunctionType

D = 2048
KC = 16
T = 512
NSLOT = 4
SLOT = 8192
EPS = 1e-6


class Sem:
    def __init__(self, nc, es, name):
        self.h = es.enter_context(nc.semaphore(name))
        self.n = 0

    def inc(self, ins, k=1):
        ins.then_inc(self.h, k)
        self.n += k
        return (self, self.n)

    def dma(self, ins):
        return self.inc(ins, 16)


def W(eng, tok):
    if tok is not None:
        eng.wait_ge(tok[0].h, tok[1])


def build(SP, SS, NS, DFF, DEPTH, NSPLIT, TAIL_SPLIT=True):
    if not TAIL_SPLIT:
        NSPLIT = 1
    nc = bass.Bass("TRN2", target_bir_lowering=False)
    NA = (DEPTH + 1) // 2
    NB = max(DEPTH // 2, 1)
    NT = SP + NS * SS
    NFB = DFF // 512
    N1P, N1S = SP // 128, SS // 128
    BLK = SP // NSPLIT

    def din(name, shape, dt=F32):
        return nc.dram_tensor(name, list(shape), dt, kind="ExternalInput").ap()

    x_prompt = din("x_prompt", [SP, D])
    x_sample = din("x_sample", [NS * SS, D])
    norm_mix = din("norm_mix", [DEPTH, D])
    a_w_in = din("a_w_in", [NA, D, 3 * D])
    a_conv_w = din("a_conv_w", [NA, 3, D])
    a_w_out = din("a_w_out", [NA, D, D])
    f_w_out = din("f_w_out", [NB, D, D])
    norm_ffn = din("norm_ffn", [DEPTH, D])
    w_up = din("w_up", [DEPTH, D, DFF])
    w_down = din("w_down", [DEPTH, DFF, D])
    final_norm = din("final_norm", [1, D])
    c_ident = din("c_ident", [128, 128])
    c_ccsc = din("c_ccsc", [256, 512])
    c_ra_p = din("c_ra_p", [N1P, 4 * N1P])
    c_ra_s = din("c_ra_s", [N1S, 4 * N1S])
    c_tw_p = din("c_tw_p", [128, 2 * N1P])
    c_tw_s = din("c_tw_s", [128, 2 * N1S])
    c_c2s2 = din("c_c2s2", [128, 256])
    c_sel = din("c_sel", [128, max(NSPLIT, 1)])
    y_prompt = nc.dram_tensor("y_prompt", [SP // NSPLIT, D], F32, kind="ExternalOutput").ap()
    y_sample = nc.dram_tensor("y_sample", [NS * SS, D], F32, kind="ExternalOutput").ap()

    R = [nc.dram_tensor("res%d" % i, [NT, D], F32).ap() for i in range(2)]
    PQ = nc.dram_tensor("pq", [32, NT, 128], BF16).ap()
    MX = nc.dram_tensor("mx", [32, NT, 64], BF16).ap()
    MXT = nc.dram_tensor("mxt", [32, SP // NSPLIT, 64], BF16).ap()
    RT = nc.dram_tensor("rtail", [SP // NSPLIT, D], F32).ap()
    wb_in = nc.dram_tensor("wb_in", [NA, 16, D, 384], BF16).ap()
    wb_aout = nc.dram_tensor("wb_aout", [NA, D, D], BF16).ap()
    wb_fout = nc.dram_tensor("wb_fout", [NB, D, D], BF16).ap()
    wb_up = nc.dram_tensor("wb_up", [DEPTH, D, DFF], BF16).ap()
    wb_down = nc.dram_tensor("wb_down", [DEPTH, DFF, D], BF16).ap()

    seqs = [(x_prompt, 0, SP, N1P, y_prompt)]
    for i in range(NS):
        seqs.append((x_sample[i * SS:(i + 1) * SS, :], SP + i * SS, SS, N1S, y_sample[i * SS:(i + 1) * SS, :]))

    es = ExitStack()
    with es:
        def sb(name, shape, dt):
            return es.enter_context(nc.sbuf_tensor(name, list(shape), dt))

        def ps(name, shape, dt):
            return es.enter_context(nc.psum_tensor(name, list(shape), dt))

        x = sb("x", [128, 4, D], F32)
        gb = sb("gb", [128, D], F32)
        junk = sb("junk", [128, D], BF16)
        hn = sb("hn", [128, D], BF16)
        hT = sb("hT", [128, KC, T + 2], BF16)
        act2 = sb("act2", [128, KC, T], BF16)
        hid = sb("hid", [128, 2, 4, T], BF16)
        tmpr = sb("tmpr", [128, 2, T], F32)
        ftmp = sb("ftmp", [128, 4, T + 2], F32)
        ring = sb("ring", [128, NSLOT * SLOT], BF16)
        io = sb("io", [128, 2 * D], BF16)
        bri = sb("bri", [128, 2, 2, 512], BF16)
        ss = sb("ss", [128, 32], F32)
        idb = sb("idb", [128, 128], BF16)
        ccsc = sb("ccsc", [128, 2, 512], BF16)
        ra_p = sb("ra_p", [128, 4 * N1P], BF16)
        ra_s = sb("ra_s", [128, 4 * N1S], BF16)
        tw_p = sb("tw_p", [128, 2 * N1P], F32)
        tw_s = sb("tw_s", [128, 2 * N1S], F32)
        c2s2 = sb("c2s2", [128, 256], BF16)
        cwt = sb("cwt", [128, NA * 3 * KC], F32)
        selt = sb("selt", [128, max(NSPLIT, 1)], F32)

        psU = [ps("psU%d" % i, [128, 512], F32) for i in range(2)]
        psD = [ps("psD%d" % i, [128, 512], F32) for i in range(3)]
        psX = ps("psX", [128, 512], F32)
        psT = [ps("psT%d" % i, [128, 8, 128], BF16) for i in range(2)]

        io_f32 = io[:, :].bitcast(F32)
        xh = io_f32

        S = {n: Sem(nc, es, n) for n in
             ["cast", "cst", "xld", "xst", "gbl", "pe", "act", "dve", "pool", "hld", "hld2", "iost", "fld", "fst"]}
        ring_ld = [Sem(nc, es, "rl%d" % i) for i in range(NSLOT)]
        PE, ACT, DVE, POOL, SY = nc.tensor, nc.scalar, nc.vector, nc.gpsimd, nc.sync
        dynv = {}

        def prow(ap, r0, n):
            return ap.rearrange("(b t) d -> b t d", t=BLK)[bass.ds(dynv["pid"], 1), r0:r0 + n, :]

        POOL.dma_start(out=idb[:], in_=c_ident[:, :]).then_inc(S["cst"].h, 16)
        POOL.dma_start(out=ccsc[:], in_=c_ccsc.rearrange("(k p) c -> p k c", p=128)).then_inc(S["cst"].h, 16)
        POOL.dma_start(out=ra_p[0:N1P, :], in_=c_ra_p[:, :]).then_inc(S["cst"].h, 16)
        POOL.dma_start(out=ra_s[0:N1S, :], in_=c_ra_s[:, :]).then_inc(S["cst"].h, 16)
        POOL.dma_start(out=c2s2[:], in_=c_c2s2[:, :]).then_inc(S["cst"].h, 16)
        SY.dma_start(out=tw_p[:], in_=c_tw_p[:, :]).then_inc(S["cst"].h, 16)
        SY.dma_start(out=tw_s[:], in_=c_tw_s[:, :]).then_inc(S["cst"].h, 16)
        SY.dma_start(out=selt[:], in_=c_sel[:, :]).then_inc(S["cst"].h, 16)
        for j in range(NA):
            for t in range(3):
                o = (j * 3 + t) * KC
                SY.dma_start(out=cwt[:, o:o + KC], in_=a_conv_w[j, t:t + 1, :].rearrange("o (c p) -> p (o c)", p=128),
                             allow_slow_non_contiguous=True).then_inc(S["cst"].h, 16)
        S["cst"].n = 16 * (8 + 3 * NA)
        cst_tok = (S["cst"], S["cst"].n)
        for e in (PE, ACT, DVE, POOL):
            W(e, cst_tok)

        cast_tok = {}

        def cast2d(key, dst, src, rows, cols):
            sem = Sem(nc, es, "c_%s%d" % key)
            for r0 in range(0, rows, 128):
                for c0 in range(0, cols, 2048):
                    c1 = min(cols, c0 + 2048)
                    t = sem.dma(POOL.dma_start(out=dst[r0:r0 + 128, c0:c1], in_=src[r0:r0 + 128, c0:c1]))
            cast_tok[key] = t

        for L in range(DEPTH):
            j = L // 2
            if L % 2 == 0:
                sem_in = Sem(nc, es, "c_in%d" % L)
                for cc in range(KC):
                    for r0 in range(0, D, 128):
                        t = sem_in.dma(POOL.dma_start(
                            out=wb_in[j, cc, r0:r0 + 128, :].rearrange("k (g c) -> k g c", g=3),
                            in_=a_w_in[j, r0:r0 + 128, :].rearrange("k (g c) -> k g c", g=3)[:, :, cc * 128:(cc + 1) * 128]))
                cast_tok[("in", L)] = t
                cast2d(("mo", L), wb_aout[j], a_w_out[j], D, D)
            else:
                cast2d(("mo", L), wb_fout[j], f_w_out[j], D, D)
            cast2d(("up", L), wb_up[L], w_up[L], D, DFF)
            cast2d(("dn", L), wb_down[L], w_down[L], DFF, D)

        plan = []
        state = {"issued": 0, "used": 0, "free_tok": {}, "cast_waited": set()}

        def ring_plan(ap, cast_key):
            plan.append((ap, cast_key))

        def ring_issue_upto(b):
            while state["issued"] <= min(b, len(plan) - 1):
                i = state["issued"]
                ap, ck = plan[i]
                slot = i % NSLOT
                if ck not in state["cast_waited"]:
                    W(SY, cast_tok[ck])
                    state["cast_waited"].add(ck)
                if i >= NSLOT:
                    W(SY, state["free_tok"][i - NSLOT])
                shp = ap.shape
                n = shp[1] * shp[2]
                dst = ring[:, slot * SLOT: slot * SLOT + n].rearrange("p (a b) -> p a b", a=shp[1])
                ring_ld[slot].dma(SY.dma_start(out=dst, in_=ap))
                state["issued"] += 1

        def ring_next():
            b = state["used"]
            ring_issue_upto(b + NSLOT - 1)
            slot = b % NSLOT
            ap, _ = plan[b]
            shp = ap.shape
            n = shp[1] * shp[2]
            view = ring[:, slot * SLOT: slot * SLOT + n].rearrange("p (a b) -> p a b", a=shp[1])
            PE.wait_ge(ring_ld[slot].h, 16 * (b // NSLOT + 1))
            state["used"] += 1
            return b, view

        def ring_free(b, tok):
            state["free_tok"][b] = tok

        tk = {k: None for k in ["x_free", "x_ld", "gb_ld", "gb_free", "hn_free", "psT0", "psT1", "hT_free", "act2_free",
                                 "io_free", "psU0", "psU1", "psD0", "psD1", "psD2", "psX", "tmpr0", "tmpr1",
                                 "hid0", "hid1", "ftmp_free", "bg_free", "u_free", "hnld_free", "bri0", "bri1", "xh_ld", "junk_free"]}
        cnt = {"psD": 0, "psU": 0}

        def load_gb(vec_ap):
            W(SY, tk["gb_free"])
            tk["gb_ld"] = S["gbl"].dma(SY.dma_start(out=gb[:], in_=vec_ap.partition_broadcast(128)))

        def transposes(src_tile, s, dst, ncols, width=128):
            last = None
            for half in range(2):
                W(PE, tk["psT%d" % half])
                for c in range(8):
                    ins = PE.transpose(psT[half][:, c, :], src_tile[:, (half * 8 + c) * 128:(half * 8 + c + 1) * 128], idb[:])
                tp = S["pe"].inc(ins)
                W(ACT, tp)
                ins = ACT.copy(dst[:, half * 8:half * 8 + 8, ncols:ncols + width], psT[half][:, :, 0:width])
                tk["psT%d" % half] = S["act"].inc(ins)
                last = tp
            return last, tk["psT1"]

        def norm_stats(nsub, srcs):
            t0 = S["dve"].inc(DVE.memset(ss[:, 8:8 + nsub], 0.0))
            W(ACT, t0)
            W(ACT, tk["x_ld"])
            W(ACT, tk["junk_free"])
            for s in range(nsub):
                ins = ACT.activation(junk[:], srcs[s], AF.Square, accum_out=ss[:, 8 + s:9 + s])
            ta = S["act"].inc(ins)
            W(DVE, ta)
            tv = S["dve"].inc(DVE.tensor_scalar(ss[:, 16:16 + nsub], ss[:, 8:8 + nsub], 1.0 / D, EPS, op0=ALU.mult, op1=ALU.add))
            W(ACT, tv)
            ta = S["act"].inc(ACT.sqrt(ss[:, 24:24 + nsub], ss[:, 16:16 + nsub]))
            W(DVE, ta)
            tv = S["dve"].inc(DVE.reciprocal(ss[:, 0:nsub], ss[:, 24:24 + nsub]))
            W(DVE, tv)
            return tv

        def norm_to_hT(halo):
            nsub = 5 if halo else 4
            srcs = [x[:, s, :] for s in range(4)] + ([xh] if halo else [])
            if halo:
                W(ACT, tk["xh_ld"])
            norm_stats(nsub, srcs)
            W(DVE, tk["gb_ld"])
            W(PE, tk["hT_free"])
            W(ACT, tk["hT_free"])
            lastE = None
            for s in range(nsub):
                W(DVE, tk["hn_free"])
                th = S["dve"].inc(DVE.scalar_tensor_tensor(hn[:], srcs[s], ss[:, s:s + 1], gb[:], op0=ALU.mult, op1=ALU.mult))
                W(PE, th)
                if s < 4:
                    tp, lastE = transposes(hn, s, hT, s * 128, 128)
                else:
                    tp, lastE = transposes(hn, s, hT, T, 2)
                tk["hn_free"] = tp
            tk["gb_free"] = th
            return lastE

        def psD_next():
            i = cnt["psD"] % 3
            cnt["psD"] += 1
            return i

        def out_proj(src, wkey, wdram):
            last = None
            for db in range(4):
                b, wv = ring_next()
                for s in range(4):
                    i = psD_next()
                    W(PE, tk["psD%d" % i])
                    for cc in range(KC):
                        ins = PE.matmul(psD[i][:], src[:, cc, s * 128:(s + 1) * 128], wv[:, cc, :], start=(cc == 0), stop=(cc == KC - 1))
                    tp = S["pe"].inc(ins)
                    W(DVE, tp)
                    ins = DVE.tensor_tensor(x[:, s, db * 512:(db + 1) * 512], psD[i][:], x[:, s, db * 512:(db + 1) * 512], ALU.add)
                    tk["psD%d" % i] = S["dve"].inc(ins)
                    last = tk["psD%d" % i]
                ring_free(b, tp)
            return tp, last

        def plan_out_proj(wdram, key):
            for db in range(4):
                ring_plan(wdram.rearrange("(cc p) d -> p cc d", p=128)[:, :, db * 512:(db + 1) * 512], key)

        def plan_mlp(L):
            up = lambda fb: ring_plan(wb_up[L].rearrange("(kc p) f -> p kc f", p=128)[:, :, fb * 512:(fb + 1) * 512], ("up", L))
            dn = lambda fb: ring_plan(wb_down[L][fb * 512:(fb + 1) * 512, :].rearrange("(fc p) d -> p fc d", p=128), ("dn", L))
            up(0)
            for fb in range(NFB):
                if fb + 1 < NFB:
                    up(fb + 1)
                dn(fb)

        def mlp(L, hT_ready):
            W(PE, hT_ready)
            res = {"last_add": None, "tp": None}

            def up_block(fb):
                hb = fb % 2
                bu, wu = ring_next()
                tsq = None
                for fc in range(4):
                    ui = cnt["psU"] % 2
                    cnt["psU"] += 1
                    W(PE, tk["psU%d" % ui])
                    for kc in range(KC):
                        ins = PE.matmul(psU[ui][:], wu[:, kc, fc * 128:(fc + 1) * 128], hT[:, kc, 0:T], start=(kc == 0), stop=(kc == KC - 1))
                    tp = S["pe"].inc(ins)
                    W(ACT, tp)
                    W(ACT, tk["tmpr%d" % ui])
                    ta = S["act"].inc(ACT.activation(tmpr[:, ui, :], psU[ui][:], AF.Relu))
                    tk["psU%d" % ui] = ta
                    W(DVE, ta)
                    if fc == 0:
                        W(DVE, tk["hid%d" % hb])
                    tsq = S["dve"].inc(DVE.tensor_tensor(hid[:, hb, fc, :], tmpr[:, ui, :], tmpr[:, ui, :], ALU.mult))
                    tk["tmpr%d" % ui] = tsq
                ring_free(bu, tp)
                res["tp_up"] = tp
                return tsq

            def down_block(fb, tsq):
                hb = fb % 2
                bd, wd = ring_next()
                W(PE, tsq)
                tp = None
                for s in range(4):
                    for db in range(4):
                        i = psD_next()
                        W(PE, tk["psD%d" % i])
                        for fc in range(4):
                            ins = PE.matmul(psD[i][:], hid[:, hb, fc, s * 128:(s + 1) * 128], wd[:, fc, db * 512:(db + 1) * 512],
                                            start=(fc == 0), stop=(fc == 3))
                        tp = S["pe"].inc(ins)
                        W(DVE, tp)
                        ins = DVE.tensor_tensor(x[:, s, db * 512:(db + 1) * 512], psD[i][:], x[:, s, db * 512:(db + 1) * 512], ALU.add)
                        tk["psD%d" % i] = S["dve"].inc(ins)
                        res["last_add"] = tk["psD%d" % i]
                ring_free(bd, tp)
                tk["hid%d" % hb] = tp
                res["tp"] = tp

            tsqs = {0: up_block(0)}
            for fb in range(NFB):
                if fb + 1 < NFB:
                    tsqs[fb + 1] = up_block(fb + 1)
                down_block(fb, tsqs[fb])
            tk["hT_free"] = res["tp"]
            return res["last_add"]

        def src_rows(L, q, r0, n):
            xin, off, Sq, N1, yout = seqs[q]
            if L == 0:
                return xin[r0:r0 + n, :]
            return R[L % 2][off + r0:off + r0 + n, :]

        def dst_rows(L, q, r0, n):
            xin, off, Sq, N1, yout = seqs[q]
            return R[(L + 1) % 2][off + r0:off + r0 + n, :]

        def load_x(L, q, g0, dyn=False):
            W(SY, tk["x_free"])
            for s in range(4):
                src = RT[g0 + s * 128:g0 + (s + 1) * 128, :] if dyn else src_rows(L, q, g0 + s * 128, 128)
                t = S["xld"].dma(SY.dma_start(out=x[:, s, :], in_=src))
            tk["x_ld"] = t

        def store_x(L, q, g0, done_tok):
            W(SY, done_tok)
            for s in range(4):
                t = S["xst"].dma(SY.dma_start(out=dst_rows(L, q, g0 + s * 128, 128), in_=x[:, s, :]))
            W(SY, t)
            tk["x_free"] = t

        def final_out(q, g0, done_tok, dyn=False):
            xin, off, Sq, N1, yout = seqs[q]
            load_gb(final_norm[0:1, :])
            W(ACT, done_tok)
            W(DVE, done_tok)
            norm_stats(4, [x[:, s, :] for s in range(4)])
            W(DVE, tk["gb_ld"])
            t = None
            for s in range(4):
                W(DVE, tk["io_free"])
                tv = S["dve"].inc(DVE.scalar_tensor_tensor(io_f32, x[:, s, :], ss[:, s:s + 1], gb[:], op0=ALU.mult, op1=ALU.mult))
                W(SY, tv)
                dsta = yout[g0 + s * 128:g0 + (s + 1) * 128, :]
                t = S["iost"].dma(SY.dma_start(out=dsta, in_=io_f32))
                tk["io_free"] = t
            tk["gb_free"] = tv
            W(SY, t)
            tk["x_free"] = t

        def conv_group(L, q, g0):
            j = L // 2
            xin, off, Sq, N1, yout = seqs[q]
            load_x(L, q, g0)
            W(DVE, tk["io_free"])
            tz = S["dve"].inc(DVE.memset(xh[:, :], 0.0))
            W(SY, tz)
            t = tz
            if g0 > 0:
                t = S["hld"].dma(SY.dma_start(out=xh[0:1, :], in_=src_rows(L, q, g0 - 1, 1)))
            if g0 + T < Sq:
                t = S["hld"].dma(SY.dma_start(out=xh[1:2, :], in_=src_rows(L, q, g0 + T, 1)))
            tk["xh_ld"] = t
            load_gb(norm_mix[L:L + 1, :])
            for cc in range(KC):
                ring_plan(wb_in[j, cc].rearrange("(kc p) c -> p kc c", p=128), ("in", L))
            plan_out_proj(wb_aout[j], ("mo", L))
            plan_mlp(L)
            hT_ready = norm_to_hT(True)
            tk["io_free"] = hT_ready
            W(PE, hT_ready)
            Bg, Cg, u, acc = ftmp[:, 0, 0:T], ftmp[:, 1, :], ftmp[:, 2, :], ftmp[:, 3, 0:T]
            psB, psC, psV, psH = psU[0], psU[1], psD[0], psX
            W(PE, tk["psD0"]); W(PE, tk["psU0"]); W(PE, tk["psU1"])
            W(DVE, tk["act2_free"])
            tpool = None
            tAB = tACm = tu = t1 = None
            hTm, hTh = hT[:, :, 0:T], hT[:, :, T:T + 2]

            def mm16(pt, wv, c0, rhs):
                for kc in range(KC):
                    ins = PE.matmul(pt, wv[:, kc, c0:c0 + 128], rhs[:, kc, :], start=(kc == 0), stop=(kc == KC - 1))
                return ins

            for cc in range(KC):
                b, wv = ring_next()
                W(PE, tAB)
                tB = S["pe"].inc(mm16(psB[:], wv, 0, hTm))
                W(PE, tACm)
                tC = S["pe"].inc(mm16(psC[:], wv, 128, hTm))
                W(PE, tu)
                mm16(psH[:, 0:2], wv, 128, hTh)
                mm16(psV[:], wv, 256, hTm)
                tp = S["pe"].inc(mm16(psH[:, 2:4], wv, 256, hTh))
                ring_free(b, tp)
                W(ACT, tB)
                W(ACT, tpool)
                tAB = S["act"].inc(ACT.copy(Bg, psB[:]))
                W(ACT, tC)
                W(ACT, tu)
                tACm = S["act"].inc(ACT.copy(Cg[:, 1:T + 1], psC[:]))
                W(ACT, tp)
                ACT.copy(Cg[:, 0:1], psH[:, 0:1])
                tACh = S["act"].inc(ACT.copy(Cg[:, T + 1:T + 2], psH[:, 1:2]))
                W(DVE, tACh)
                W(DVE, tpool)
                DVE.tensor_tensor(u[:, 1:T + 1], psV[:], Cg[:, 1:T + 1], ALU.mult)
                DVE.tensor_tensor(u[:, 0:1], psH[:, 2:3], Cg[:, 0:1], ALU.mult)
                tu = S["dve"].inc(DVE.tensor_tensor(u[:, T + 1:T + 2], psH[:, 3:4], Cg[:, T + 1:T + 2], ALU.mult))
                o = (j * 3) * KC + cc
                W(DVE, tu)
                t1 = S["dve"].inc(DVE.tensor_scalar(acc, u[:, 0:T], cwt[:, o:o + 1], None, op0=ALU.mult))
                W(DVE, t1)
                t1 = S["dve"].inc(DVE.scalar_tensor_tensor(acc, u[:, 1:T + 1], cwt[:, o + KC:o + KC + 1], acc, op0=ALU.mult, op1=ALU.add))
                W(DVE, t1)
                t1 = S["dve"].inc(DVE.scalar_tensor_tensor(acc, u[:, 2:T + 2], cwt[:, o + 2 * KC:o + 2 * KC + 1], acc, op0=ALU.mult, op1=ALU.add))
                W(DVE, t1)
                W(DVE, tAB)
                tpool = S["dve"].inc(DVE.tensor_tensor(act2[:, cc, :], acc, Bg, ALU.mult))
            tk["psU0"] = tAB; tk["psU1"] = tACm; tk["psD0"] = tu; tk["psX"] = tu
            tk["hT_free"] = tp
            W(PE, tpool)
            W(DVE, tk["x_ld"])
            tp, last = out_proj(act2, ("mo", L), wb_aout[j])
            tk["act2_free"] = tp
            return last

        def ffn_part(L, q, g0, mix_done, dyn=False):
            load_gb(norm_ffn[L:L + 1, :])
            W(ACT, mix_done)
            W(DVE, mix_done)
            hT_ready = norm_to_hT(False)
            last_add = mlp(L, hT_ready)
            if L == DEPTH - 1:
                final_out(q, g0, last_add, dyn)
            else:
                done = last_add
                if (L + 1) % 2 == 1:
                    done = fourier_a_body(L + 1, q, g0, last_add)
                store_x(L, q, g0, done)

        def fourier_a_body(L, q, g0, x_done):
            xin, off, Sq, N1, yout = seqs[q]
            load_gb(norm_mix[L:L + 1, :])
            W(ACT, x_done)
            W(DVE, x_done)
            hT_ready = norm_to_hT(False)
            x_read_done = tk["gb_free"]
            W(PE, hT_ready)
            iov = io[:, :].rearrange("p (cb h c) -> p cb h c", cb=32, h=2)
            tp = None
            for s in range(4):
                W(ACT, tk["io_free"])
                for g8 in range(8):
                    ui = cnt["psU"] % 2
                    cnt["psU"] += 1
                    W(PE, tk["psU%d" % ui])
                    for h in range(2):
                        for kk in range(2):
                            ins = PE.matmul(psU[ui][:, h * 256:(h + 1) * 256], hT[:, 2 * g8 + kk, s * 128:(s + 1) * 128],
                                            ccsc[:, kk, h * 256:(h + 1) * 256], start=(kk == 0), stop=(kk == 1))
                    tp = S["pe"].inc(ins)
                    W(ACT, tp)
                    ins = ACT.copy(iov[:, 4 * g8:4 * g8 + 4, :, :].rearrange("p cb h c -> p h cb c"),
                                   psU[ui][:].rearrange("p (h cb c) -> p h cb c", h=2, cb=4))
                    tk["psU%d" % ui] = S["act"].inc(ins)
                W(SY, tk["psU%d" % ui])
                r0 = off + g0 + s * 128
                t = S["iost"].dma(SY.dma_start(out=PQ[:, r0:r0 + 128, :].rearrange("cb t c -> t cb c"),
                                               in_=io[:, :].rearrange("p (cb c) -> p cb c", cb=32)))
                tk["io_free"] = t
            tk["hT_free"] = tp
            return x_read_done

        def fft_unit(q, cb):
            xin, off, Sq, N1, yout = seqs[q]
            ra = ra_p if N1 == N1P else ra_s
            tw = tw_p if N1 == N1P else tw_s
            if N1P == N1S:
                ra, tw = ra_p, tw_p
            nchA = 256 // N1
            nchC = 2 * nchA
            pz = ring[:, 0:2 * SLOT].rearrange("p (n h c) -> p n h c", n=128, h=2)
            mxv = ring[:, 2 * SLOT:2 * SLOT + N1 * 64].rearrange("p (k c) -> p k c", c=64)
            W(SY, tk["fft_pz_free"])
            t = S["fld"].dma(SY.dma_start(out=pz[0:N1], in_=PQ[cb, off:off + Sq, :].rearrange("(n1 n2) (h c) -> n1 n2 h c", n2=128, h=2)))
            W(PE, t)
            trb = tw[:, 0:N1].unsqueeze(1).unsqueeze(1).broadcast_to([128, nchA, 2, N1])
            tib = tw[:, N1:2 * N1].unsqueeze(1).unsqueeze(1).broadcast_to([128, nchA, 2, N1])
            t1v = ftmp[:, 0, 0:512].rearrange("p (c r k) -> p c r k", c=nchA, r=2)
            t2v = ftmp[:, 1, 0:512].rearrange("p (c r k) -> p c r k", c=nchA, r=2)
            W(ACT, tk["fft_mx_free"])
            tp = None
            for ci, c0 in enumerate(range(0, 64, nchC)):
                bset = ci % 2
                tpool = None
                for hh in range(2):
                    ui = cnt["psU"] % 2
                    cnt["psU"] += 1
                    W(PE, tk["psU%d" % ui])
                    for jc in range(nchA):
                        ch = c0 + hh * nchA + jc
                        PE.matmul(psU[ui][:, jc * 2 * N1:(jc + 1) * 2 * N1], pz[0:N1, :, 0, ch], ra[0:N1, 0:2 * N1], start=True, stop=False)
                        ins = PE.matmul(psU[ui][:, jc * 2 * N1:(jc + 1) * 2 * N1], pz[0:N1, :, 1, ch], ra[0:N1, 2 * N1:4 * N1], start=False, stop=True)
                    tp = S["pe"].inc(ins)
                    W(DVE, tp)
                    W(DVE, tpool if tpool is not None else tk["ftmp_free"])
                    av = psU[ui][:].rearrange("p (c r k) -> p c r k", c=nchA, r=2)
                    DVE.tensor_tensor(t1v, av, trb, ALU.mult)
                    tv = S["dve"].inc(DVE.tensor_tensor(t2v, av, tib, ALU.mult))
                    tk["psU%d" % ui] = tv
                    W(DVE, tv)
                    if hh == 0:
                        W(DVE, tk["bri%d" % bset])
                    brv = bri[:, bset, 0, hh * 256:(hh + 1) * 256].rearrange("p (c k) -> p c k", c=nchA)
                    biv = bri[:, bset, 1, hh * 256:(hh + 1) * 256].rearrange("p (c k) -> p c k", c=nchA)
                    DVE.tensor_tensor(brv, t1v[:, :, 0, :], t2v[:, :, 1, :], ALU.subtract)
                    tpool = S["dve"].inc(DVE.tensor_tensor(biv, t2v[:, :, 0, :], t1v[:, :, 1, :], ALU.add))
                    tk["ftmp_free"] = tpool
                i = psD_next()
                W(PE, tk["psD%d" % i])
                W(PE, tpool)
                PE.matmul(psD[i][:], c2s2[:, 0:128], bri[:, bset, 0, :], start=True, stop=False)
                ins = PE.matmul(psD[i][:], c2s2[:, 128:256], bri[:, bset, 1, :], start=False, stop=True)
                tpc = S["pe"].inc(ins)
                tk["bri%d" % bset] = tpc
                W(ACT, tpc)
                ins = ACT.copy(mxv[:, :, c0:c0 + nchC].rearrange("p k c -> p c k"), psD[i][:].rearrange("p (c k) -> p c k", c=nchC))
                tk["psD%d" % i] = S["act"].inc(ins)
            tk["fft_pz_free"] = tp
            W(SY, tk["psD%d" % i])
            t = S["fst"].dma(SY.dma_start(out=MX[cb, off:off + Sq, :].rearrange("(k2 k1) c -> k2 k1 c", k1=N1), in_=mxv))
            tk["fft_mx_free"] = t
            return t

        def fourier_b_group(L, q, g0, dyn=False):
            j = L // 2
            xin, off, Sq, N1, yout = seqs[q]
            load_x(L, q, g0, dyn)
            plan_out_proj(wb_fout[j], ("mo", L))
            plan_mlp(L)
            W(PE, tk["act2_free"])
            W(ACT, tk["act2_free"])
            lastE = None
            bufs = [hn, junk]
            bfree = [tk["hn_free"], tk["junk_free"]]
            ld = [None] * 4

            def issue(s):
                r0 = off + g0 + s * 128
                if dyn:
                    srcm = MXT[:, g0 + s * 128:g0 + (s + 1) * 128, :].rearrange("cb t c -> t cb c")
                else:
                    srcm = MX[:, r0:r0 + 128, :].rearrange("cb t c -> t cb c")
                W(SY, bfree[s % 2])
                ld[s] = S["hld" if s % 2 == 0 else "hld2"].dma(
                    SY.dma_start(out=bufs[s % 2][:, :].rearrange("p (cb c) -> p cb c", cb=32), in_=srcm))

            issue(0)
            for s in range(4):
                if s + 1 < 4:
                    issue(s + 1)
                W(PE, ld[s])
                tp, lastE = transposes(bufs[s % 2], s, act2, s * 128, 128)
                bfree[s % 2] = tp
            tk["hn_free"] = bfree[0]
            tk["junk_free"] = bfree[1]
            W(PE, lastE)
            W(DVE, tk["x_ld"])
            tp, last = out_proj(act2, ("mo", L), wb_fout[j])
            tk["act2_free"] = tp
            return last

        tk["fft_pz_free"] = None
        tk["fft_mx_free"] = None

        def select_tail(L):
            def sel_pass(ntiles, cand, dst, stage, accs, free_toks):
                st_free = free_toks
                acc_free = [None] * len(accs)
                tlast = None
                for i in range(ntiles):
                    acc = accs[i % len(accs)]
                    for b in range(NSPLIT):
                        for tkn in st_free:
                            W(SY, tkn)
                        tl = S["fld"].dma(SY.dma_start(out=stage, in_=cand(i, b)))
                        W(DVE, tl)
                        if b == 0:
                            W(DVE, acc_free[i % len(accs)])
                            tv = S["dve"].inc(DVE.tensor_scalar(acc, stage, selt[:, 0:1], None, op0=ALU.mult))
                        else:
                            W(DVE, tv)
                            tv = S["dve"].inc(DVE.scalar_tensor_tensor(acc, stage, selt[:, b:b + 1], acc, op0=ALU.mult, op1=ALU.add))
                        st_free = [tv]
                    W(SY, tv)
                    tlast = S["fst"].dma(SY.dma_start(out=dst(i), in_=acc))
                    acc_free = [tlast] * len(accs)
                return tlast, tv

            src = R[L % 2]
            for tkn in (tk["x_free"], tk["io_free"], tk["hn_free"]):
                W(DVE, tkn)
            t1, tv1 = sel_pass(BLK // 128,
                               lambda i, b: src[b * BLK + i * 128:b * BLK + (i + 1) * 128, :],
                               lambda i: RT[i * 128:(i + 1) * 128, :],
                               io_f32, [x[:, k, :] for k in range(4)], [tk["io_free"], tk["x_free"]])
            t2, tv2 = sel_pass(32,
                               lambda i, b: MX[i, b * BLK:(b + 1) * BLK, :].rearrange("(p r) c -> p (r c)", p=128),
                               lambda i: MXT[i].rearrange("(p r) c -> p (r c)", p=128),
                               junk[:, 0:(BLK // 128) * 64], [hn[:, 0:(BLK // 128) * 64], hn[:, 1024:1024 + (BLK // 128) * 64]], [tk["hn_free"]])
            W(SY, t1)
            W(SY, t2)
            tk["x_free"] = t1
            tk["io_free"] = tv1
            tk["hn_free"] = t2

        def drain_ring_for_fft():
            n = state["used"]
            for i in range(max(0, n - NSLOT), n):
                W(SY, state["free_tok"][i])

        groups = [(q, g0) for q in range(len(seqs)) for g0 in range(0, seqs[q][2], T)]
        for L in range(DEPTH):
            if L % 2 == 0:
                for (q, g0) in groups:
                    mix_done = conv_group(L, q, g0)
                    ffn_part(L, q, g0, mix_done)
            else:
                SY.wait_ge(S["iost"].h, S["iost"].n)
                drain_ring_for_fft()
                tlast = None
                for q in range(len(seqs)):
                    for cb in range(32):
                        tlast = fft_unit(q, cb)
                W(SY, tlast)
                W(PE, tlast)
                if L == DEPTH - 1 and NSPLIT > 1:
                    gl = [(0, g0, True) for g0 in range(0, BLK, T)] + [(q, g0, False) for (q, g0) in groups if q > 0]
                    select_tail(L)
                else:
                    gl = [(q, g0, False) for (q, g0) in groups]
                for (q, g0, dyn) in gl:
                    mix_done = fourier_b_group(L, q, g0, dyn)
                    ffn_part(L, q, g0, mix_done, dyn)
        W(SY, tk["x_free"])
        for nm in ("iost", "xst", "fst"):
            if S[nm].n:
                SY.wait_ge(S[nm].h, S[nm].n)
    return nc


def _consts(SP, SS):
    def ra(n1):
        n = np.arange(n1)
        a = 2 * np.pi * np.outer(n, n) / n1
        c, s = np.cos(a), np.sin(a)
        return (np.concatenate([c, -s, -s, -c], 1) / np.sqrt(n1)).astype(np.float32)

    def tw(n1):
        N = n1 * 128
        a = 2 * np.pi * np.outer(np.arange(128), np.arange(n1)) / N
        return np.concatenate([np.cos(a), -np.sin(a)], 1).astype(np.float32)

    a = 2 * np.pi * np.outer(np.arange(256), np.arange(256)) / 256
    ccsc = (np.concatenate([np.cos(a), np.sin(a)], 1) / 16.0).astype(np.float32)
    a = 2 * np.pi * np.outer(np.arange(128), np.arange(128)) / 128
    c2s2 = (np.concatenate([np.cos(a), np.sin(a)], 1) / np.sqrt(128.0)).astype(np.float32)
    return {"c_ident": np.eye(128, dtype=np.float32), "c_ccsc": ccsc, "c_ra_p": ra(SP // 128), "c_ra_s": ra(SS // 128),
            "c_tw_p": tw(SP // 128), "c_tw_s": tw(SS // 128), "c_c2s2": c2s2}


def run(inputs, SP, SS, NS, DFF, DEPTH, n_cores):
    nc = build(SP, SS, NS, DFF, DEPTH, n_cores)
    consts = _consts(SP, SS)
    xs = np.ascontiguousarray(inputs["x_sample"]).reshape(-1, D)
    in_maps = []
    for c in range(n_cores):
        m = {k: np.ascontiguousarray(inputs[k]) for k in
             ["norm_mix", "a_w_in", "a_conv_w", "a_w_out", "f_w_out", "norm_ffn", "w_up", "w_down"]}
        m["final_norm"] = np.ascontiguousarray(inputs["final_norm"]).reshape(1, D)
        m["x_prompt"] = np.ascontiguousarray(inputs["x_prompt"]).reshape(SP, D)
        m["x_sample"] = xs[c * NS * SS:(c + 1) * NS * SS]
        m.update(consts)
        sel = np.zeros((128, n_cores), np.float32)
        sel[:, c] = 1.0
        m["c_sel"] = sel
        in_maps.append(m)
    res = run_bass_kernel_spmd(nc, in_maps, core_ids=list(range(n_cores)))
    blk = SP // n_cores
    yp = np.concatenate([res.results[c]["y_prompt"] for c in range(n_cores)], 0).reshape(1, SP, D)
    ys = np.concatenate([res.results[c]["y_sample"] for c in range(n_cores)], 0).reshape(n_cores * NS, SS, D)
    return yp.astype(np.float32), ys.astype(np.float32)


def kernel(x_prompt, x_sample, norm_mix, a_w_in, a_conv_w, a_w_out, f_w_out, norm_ffn, w_up, w_down, final_norm):
    inputs = dict(x_prompt=x_prompt, x_sample=x_sample, norm_mix=norm_mix, a_w_in=a_w_in, a_conv_w=a_conv_w,
                  a_w_out=a_w_out, f_w_out=f_w_out, norm_ffn=norm_ffn, w_up=w_up, w_down=w_down, final_norm=final_norm)
    inputs = {k: np.asarray(v) for k, v in inputs.items()}
    return run(inputs, SP=16384, SS=2048, NS=2, DFF=8192, DEPTH=4, n_cores=8)
```
